# Optimizing a Trainium2 kernel written in Bass

```python
import jax, jax.numpy as jnp
from jax import lax
from jax.lax import linalg as lax_linalg
import numpy as np

D_MODEL = 2048
BATCH = 16
SEQ = 2048
DEPTH = 2

MIX_WIDTH = D_MODEL
CONV_CH = MIX_WIDTH // 2
CONV_GROUPS = 8
CONV_K = 31
GDN_HEAD_DIM = 128
GDN_V_HEADS = (MIX_WIDTH // 2) // GDN_HEAD_DIM
GDN_QK_HEADS = GDN_V_HEADS // 2
GDN_K_WIDTH = GDN_QK_HEADS * GDN_HEAD_DIM
GDN_V_WIDTH = GDN_V_HEADS * GDN_HEAD_DIM
SHORT_CONV_K = 4
CHUNK = 64
D_FF = ((8 * D_MODEL // 3 + 255) // 256) * 256
FFN_CONV_K = 3
EPS = 1e-6
IN_WIDTH = 2 * CONV_CH + 2 * GDN_K_WIDTH + 2 * GDN_V_WIDTH + 2 * GDN_V_HEADS

kernel_name = "hymba_style_conformer_conv_gated_deltanet_convffn"


def rms_norm(x, g):
    xf = x.astype(jnp.float32)
    y = xf * lax.rsqrt(jnp.mean(xf * xf, axis=-1, keepdims=True) + EPS)
    return (y * g.astype(jnp.float32)).astype(x.dtype)


def group_layer_norm(x, g, b, groups):
    B, S, C = x.shape
    xf = x.astype(jnp.float32).reshape(B, S, groups, C // groups)
    mu = jnp.mean(xf, axis=-1, keepdims=True)
    xc = xf - mu
    var = jnp.mean(xc * xc, axis=-1, keepdims=True)
    y = (xc * lax.rsqrt(var + EPS)).reshape(B, S, C)
    return (y * g.astype(jnp.float32) + b.astype(jnp.float32)).astype(x.dtype)


def causal_depthwise_conv(x, w):
    K, C = w.shape
    return lax.conv_general_dilated(
        x, w[:, None, :].astype(x.dtype), window_strides=(1,), padding=[(K - 1, 0)],
        dimension_numbers=("NWC", "WIO", "NWC"), feature_group_count=C)


def l2_normalize(t):
    return t * lax.rsqrt(jnp.sum(t * t, axis=-1, keepdims=True) + EPS)


def chunk_gated_delta_rule(q, k, v, g, beta):
    B, S, H, dk = q.shape
    dv = v.shape[-1]
    N = S // CHUNK
    q = q * (dk ** -0.5)

    def to_chunks(t):
        return t.reshape(B, N, CHUNK, H, t.shape[-1]).transpose(0, 3, 1, 2, 4)

    q, k, v = to_chunks(q), to_chunks(k), to_chunks(v)
    g = g.reshape(B, N, CHUNK, H).transpose(0, 3, 1, 2)
    beta = beta.reshape(B, N, CHUNK, H).transpose(0, 3, 1, 2)
    g = jnp.cumsum(g, axis=-1)

    causal = jnp.tril(jnp.ones((CHUNK, CHUNK), dtype=bool))
    strict = jnp.tril(jnp.ones((CHUNK, CHUNK), dtype=bool), -1)
    decay = jnp.exp(jnp.where(causal, g[..., :, None] - g[..., None, :], -jnp.inf))

    k_beta = k * beta[..., None]
    v_beta = v * beta[..., None]
    L = jnp.where(strict, jnp.einsum("bhnid,bhnjd->bhnij", k_beta, k) * decay, 0.0)
    A = L + jnp.eye(CHUNK, dtype=L.dtype)
    rhs = jnp.concatenate([v_beta, k_beta * jnp.exp(g)[..., None]], axis=-1)
    sol = lax_linalg.triangular_solve(A, rhs, left_side=True, lower=True, unit_diagonal=True)
    u, w = sol[..., :dv], sol[..., dv:]
    qk = jnp.einsum("bhnid,bhnjd->bhnij", q, k) * decay

    def step(state, inp):
        q_c, k_c, u_c, w_c, g_c, qk_c = inp
        v_new = u_c - jnp.einsum("bhck,bhkv->bhcv", w_c, state)
        o_c = jnp.einsum("bhck,bhkv->bhcv", q_c * jnp.exp(g_c)[..., None], state) + \
            jnp.einsum("bhij,bhjv->bhiv", qk_c, v_new)
        g_last = g_c[..., -1]
        k_dec = k_c * jnp.exp(g_last[..., None] - g_c)[..., None]
        state = state * jnp.exp(g_last)[..., None, None] + jnp.einsum("bhck,bhcv->bhkv", k_dec, v_new)
        return state, o_c

    xs = tuple(jnp.moveaxis(t, 2, 0) for t in (q, k, u, w, g, qk))
    state0 = jnp.zeros((B, H, dk, dv), jnp.float32)
    _, o = lax.scan(step, state0, xs)
    return o.transpose(1, 0, 3, 2, 4).reshape(B, S, H, dv)


def conformer_conv_group(a_val, a_gate, dw_w, dw_b, ln_g, ln_b, pw_w, pw_b):
    u = a_val * jax.nn.sigmoid(a_gate)
    u = causal_depthwise_conv(u, dw_w) + dw_b
    u = group_layer_norm(u, ln_g, ln_b, CONV_GROUPS)
    u = jax.nn.silu(u)
    return u @ pw_w + pw_b


def gated_deltanet_group(q, k, v, z, b_raw, a_raw, conv_w, a_log, dt_bias, norm_g):
    B, S, _ = q.shape
    dtype = q.dtype
    qkv = jax.nn.silu(causal_depthwise_conv(jnp.concatenate([q, k, v], axis=-1), conv_w))
    q, k, v = jnp.split(qkv, [GDN_K_WIDTH, 2 * GDN_K_WIDTH], axis=-1)
    rep = GDN_V_HEADS // GDN_QK_HEADS
    q = jnp.repeat(q.reshape(B, S, GDN_QK_HEADS, GDN_HEAD_DIM), rep, axis=2).astype(jnp.float32)
    k = jnp.repeat(k.reshape(B, S, GDN_QK_HEADS, GDN_HEAD_DIM), rep, axis=2).astype(jnp.float32)
    v = v.reshape(B, S, GDN_V_HEADS, GDN_HEAD_DIM).astype(jnp.float32)
    q, k = l2_normalize(q), l2_normalize(k)
    beta = jax.nn.sigmoid(b_raw.astype(jnp.float32))
    g = -jnp.exp(a_log.astype(jnp.float32)) * jax.nn.softplus(
        a_raw.astype(jnp.float32) + dt_bias.astype(jnp.float32))
    o = chunk_gated_delta_rule(q, k, v, g, beta)
    o = o * lax.rsqrt(jnp.mean(o * o, axis=-1, keepdims=True) + EPS) * norm_g.astype(jnp.float32)
    o = o * jax.nn.silu(z.reshape(B, S, GDN_V_HEADS, GDN_HEAD_DIM).astype(jnp.float32))
    return o.reshape(B, S, GDN_V_WIDTH).astype(dtype)


def setup_inputs(seed: int = 0) -> dict:
    key = jax.random.key(seed)
    ks = jax.random.split(key, 24)
    L = DEPTH

    def nrm(k, shape, scale):
        return jax.random.normal(k, shape, jnp.float32) * scale

    dt = jnp.exp(jax.random.uniform(ks[10], (L, GDN_V_HEADS), jnp.float32,
                                    np.log(1e-3).astype(np.float32), np.log(1e-1).astype(np.float32)))
    return {
        "x": nrm(ks[0], (BATCH, SEQ, D_MODEL), 1.0),
        "mix_norm_g": 1.0 + nrm(ks[1], (L, D_MODEL), 0.02),
        "w_in": nrm(ks[2], (L, D_MODEL, IN_WIDTH), D_MODEL ** -0.5),
        "conv_dw_w": nrm(ks[3], (L, CONV_K, CONV_CH), CONV_K ** -0.5),
        "conv_dw_b": nrm(ks[4], (L, CONV_CH), 0.02),
        "conv_ln_g": 1.0 + nrm(ks[5], (L, CONV_CH), 0.02),
        "conv_ln_b": nrm(ks[6], (L, CONV_CH), 0.02),
        "conv_pw_w": nrm(ks[7], (L, CONV_CH, CONV_CH), CONV_CH ** -0.5),
        "conv_pw_b": nrm(ks[8], (L, CONV_CH), 0.02),
        "gdn_conv_w": nrm(ks[9], (L, SHORT_CONV_K, 2 * GDN_K_WIDTH + GDN_V_WIDTH), SHORT_CONV_K ** -0.5),
        "gdn_a_log": jnp.log(jax.random.uniform(ks[11], (L, GDN_V_HEADS), jnp.float32, 1.0, 16.0)),
        "gdn_dt_bias": dt + jnp.log(-jnp.expm1(-dt)),
        "gdn_norm_g": 1.0 + nrm(ks[12], (L, GDN_HEAD_DIM), 0.02),
        "w_out": nrm(ks[13], (L, MIX_WIDTH, D_MODEL), MIX_WIDTH ** -0.5),
        "ffn_norm_g": 1.0 + nrm(ks[14], (L, D_MODEL), 0.02),
        "w_up": nrm(ks[15], (L, D_MODEL, 2 * D_FF), D_MODEL ** -0.5),
        "ffn_conv_w": nrm(ks[16], (L, FFN_CONV_K, D_FF), FFN_CONV_K ** -0.5),
        "ffn_conv_b": nrm(ks[17], (L, D_FF), 0.02),
        "w_down": nrm(ks[18], (L, D_FF, D_MODEL), D_FF ** -0.5),
        "final_norm_g": 1.0 + nrm(ks[19], (D_MODEL,), 0.02),
    }


def reference(x, mix_norm_g, w_in, conv_dw_w, conv_dw_b, conv_ln_g, conv_ln_b, conv_pw_w, conv_pw_b,
              gdn_conv_w, gdn_a_log, gdn_dt_bias, gdn_norm_g, w_out, ffn_norm_g, w_up,
              ffn_conv_w, ffn_conv_b, w_down, final_norm_g):
    splits = np.cumsum([CONV_CH, CONV_CH, GDN_K_WIDTH, GDN_K_WIDTH, GDN_V_WIDTH, GDN_V_WIDTH,
                        GDN_V_HEADS]).tolist()
    for l in range(DEPTH):
        h = rms_norm(x, mix_norm_g[l])
        p = h @ w_in[l]
        a_val, a_gate, q, k, v, z, b_raw, a_raw = jnp.split(p, splits, axis=-1)
        out_a = conformer_conv_group(a_val, a_gate, conv_dw_w[l], conv_dw_b[l], conv_ln_g[l],
                                     conv_ln_b[l], conv_pw_w[l], conv_pw_b[l])
        out_b = gated_deltanet_group(q, k, v, z, b_raw, a_raw, gdn_conv_w[l], gdn_a_log[l],
                                     gdn_dt_bias[l], gdn_norm_g[l])
        x = x + jnp.concatenate([out_a, out_b], axis=-1) @ w_out[l]
        h = rms_norm(x, ffn_norm_g[l])
        gate, up = jnp.split(h @ w_up[l], 2, axis=-1)
        gate = causal_depthwise_conv(gate, ffn_conv_w[l]) + ffn_conv_b[l]
        x = x + (jax.nn.silu(gate) * up) @ w_down[l]
    return rms_norm(x, final_norm_g)
```

```python
import numpy as np
from contextlib import ExitStack
import concourse.bass as bass
import concourse.mybir as mybir
from concourse.bass_utils import run_bass_kernel_spmd

F32 = mybir.dt.float32
BF16 = mybir.dt.bfloat16
AF = mybir.ActivationFunctionType
OP = mybir.AluOpType

D = 2048
CC = 1024
INW = 5136
DFF = 5632
NFC = 44
EPS = 1e-6
NCORES = 8
CFG = dict(NSEQ=2, S=2048, T=256, DEPTH=2)
SAME_ENGINE_SYNC = True
EPOCH = 30000
KD = 19

O_MIXG, O_FFNG, O_FING = 0, 16, 32
O_DWW = 48
O_DWB = O_DWW + 248
O_LNG = O_DWB + 8
O_LNB = O_LNG + 8
O_PWB = O_LNB + 8
O_GCW = O_PWB + 8
O_ALOG = O_GCW + 64
O_DTB = O_ALOG + 8
O_GNG = O_DTB + 8
O_FCW = O_GNG + 1
O_FCB = O_FCW + 132
NCV = O_FCB + 44
K_ID, K_TRI, K_ONE, K_MNEG, K_STR, K_IND = 0, 128, 256, 384, 512, 640
NK = 640 + 1024


class Buf:
    __slots__ = ("w", "r", "ctr")

    def __init__(self):
        self.w = None
        self.r = {}
        self.ctr = None


class Inst:
    __slots__ = ("eng", "fn", "waits", "signal", "dma", "idx", "sem", "val", "ctr")

    def __init__(self, eng, fn, dma):
        self.eng, self.fn, self.dma = eng, fn, dma
        self.waits = []
        self.signal = False
        self.sem = None
        self.val = None
        self.ctr = None


ENGS = ("pe", "act", "dve", "pool", "sp")


class Prog:
    def __init__(self):
        self.q = {e: [] for e in ENGS}
        self.known = {e: {} for e in ENGS}
        self.kdma = {e: set() for e in ENGS}
        self.dmas = []

    def _wait(self, inst, d):
        if d is inst or d is None:
            return
        e = inst.eng
        if d.dma:
            if id(d) in self.kdma[e]:
                return
            self.kdma[e].add(id(d))
        else:
            if d.eng == e and (e == "pe" or not SAME_ENGINE_SYNC):
                return
            if self.known[e].get(d.eng, -1) >= d.idx:
                return
            self.known[e][d.eng] = d.idx
            d.signal = True
        inst.waits.append(d)

    def op(self, eng, fn, r=(), w=(), dma=False, ctr=None):
        inst = Inst(eng, fn, dma)
        inst.idx = len(self.q[eng])
        inst.ctr = ctr
        if dma:
            self.dmas.append(inst)
        for b in r:
            self._wait(inst, b.w)
        for b in w:
            self._wait(inst, b.w)
            for d in b.r.values():
                self._wait(inst, d)
        key = ("d", id(inst)) if dma else eng
        for b in r:
            b.r[key] = inst
        for b in w:
            b.w = inst
            b.r = {}
        self.q[eng].append(inst)
        return inst

    def barrier(self, extra=(), sp=True):
        last = {}
        for e in ENGS:
            last[e] = None
            for i in reversed(self.q[e]):
                if i.fn is not None:
                    last[e] = i
                    break
        for e in ENGS:
            if e == "sp" and not sp:
                continue
            inst = Inst(e, None, False)
            inst.idx = len(self.q[e])
            for e2 in ("pe", "act", "dve", "pool"):
                d = last[e2]
                if d is not None and e2 != e and not d.dma:
                    self._wait(inst, d)
            for d in extra:
                self._wait(inst, d)
            self.q[e].append(inst)


class Ring:
    def __init__(self, items):
        self.items = items
        self.i = 0

    def next(self):
        it = self.items[self.i % len(self.items)]
        self.i += 1
        return it

    def take(self, n=1):
        it = self.next()
        assert it.busy == 0, "ring slot re-allocated while a deferred user is pending"
        it.busy = n
        return it


class Slot:
    __slots__ = ("ap", "buf", "busy")

    def __init__(self, ap):
        self.ap = ap
        self.buf = Buf()
        self.busy = 0

    def free(self):
        assert self.busy > 0
        self.busy -= 1


def build(cfg):
    NSEQ, S, T, DEPTH = cfg["NSEQ"], cfg["S"], cfg["T"], cfg["DEPTH"]
    NB = T // 128
    NT = S // T
    nc = bass.Bass("TRN2", target_bir_lowering=False)
    P = Prog()
    es = ExitStack()

    def dram(name, shape, dt, kind):
        return nc.dram_tensor(name, list(shape), dt, kind=kind).ap()

    x = dram("x", [NSEQ, S, D], F32, "ExternalInput")
    NOW = cfg.get("STOP", 99) <= -2
    WIN_ONLY = cfg.get("WSET", "all") == "in"
    if not NOW:
        w_in = dram("w_in", [DEPTH, D, INW], F32, "ExternalInput")
    if not NOW and not WIN_ONLY:
        w_pw = dram("w_pw", [DEPTH, CC, CC], F32, "ExternalInput")
        w_out = dram("w_out", [DEPTH, D, D], F32, "ExternalInput")
        w_up = dram("w_up", [DEPTH, D, 2 * DFF], F32, "ExternalInput")
        w_dn = dram("w_dn", [DEPTH, DFF, D], F32, "ExternalInput")
    cvec = dram("cvec", [DEPTH, 128, NCV], F32, "ExternalInput")
    consts = dram("consts", [128, NK], F32, "ExternalInput")
    out = dram("out", [NSEQ, S, D], F32, "ExternalOutput")
    win_s = dram("win_s", [DEPTH, 40, 128, 2048], BF16, "Internal")
    wba_s = dram("wba_s", [DEPTH, 128, 256], BF16, "Internal")
    wpw_s = dram("wpw_s", [DEPTH, 8, 128, 1024], BF16, "Internal")
    wout_s = dram("wout_s", [DEPTH, 16, 128, 2048], BF16, "Internal")
    wup_s = dram("wup_s", [DEPTH, 88, 128, 2048], BF16, "Internal")
    wdn_s = dram("wdn_s", [DEPTH, 16, 128, DFF], BF16, "Internal")
    wdg_s = dram("wdg_s", [DEPTH, 8, 128, 31 * 128], BF16, "Internal")

    def sb(name, shape, dt):
        t = es.enter_context(nc.sbuf_tensor(name, list(shape), dt))
        return t.ap() if hasattr(t, "ap") else t[:]

    xT = sb("xT", [128, 16, T], F32)
    xTb = [Buf() for _ in range(16)]
    hT = sb("hT", [128, 16, T], BF16)
    hTb = [Buf() for _ in range(16)]
    mixcat = sb("mixcat", [128, 16, T], BF16)
    mixb = [Buf() for _ in range(16)]
    actA = sb("actA", [128, 8, T], BF16)
    actAb = [Buf() for _ in range(8)]
    cst = sb("cst", [128, NK], F32)
    cstb = Buf()
    cvt = sb("cvt", [128, DEPTH, NCV], F32)
    cvtb = Buf()
    onesb = sb("onesb", [128, 128], BF16)
    negA = sb("negA", [128, DEPTH, 8], F32)
    histu = sb("histu", [128, DEPTH, 8, 30], BF16)
    histq = sb("histq", [128, DEPTH, 16, 3], F32)
    histf = sb("histf", [128, DEPTH, NFC, 2], F32)
    histub = [[Buf() for _ in range(8)] for _ in range(DEPTH)]
    histqb = [[Buf() for _ in range(16)] for _ in range(DEPTH)]
    histfb = [[Buf() for _ in range(NFC)] for _ in range(DEPTH)]
    Sst = sb("Sst", [128, DEPTH, 8, 128], F32)
    Sb = [[Buf() for _ in range(8)] for _ in range(DEPTH)]
    wring = Ring([Slot(sb(f"wsl{i}", [128, 4096], BF16)) for i in range(5)])
    wbar = Ring([Slot(sb(f"wba{i}", [128, 256], BF16)) for i in range(2)])
    tmpF = Ring([Slot(sb(f"tF{i}", [128, T], F32)) for i in range(8)])
    tmpB = Ring([Slot(sb(f"tB{i}", [128, T], BF16)) for i in range(8)])
    cring = Ring([Slot(sb(f"cb{i}", [128, T + 32], F32)) for i in range(5)])
    ubring = Ring([Slot(sb(f"ub{i}", [128, T + 32], BF16)) for i in range(4)])
    ringAD = Ring([Slot(sb(f"rAD{i}", [128, T], F32)) for i in range(4)])
    ringACC = Ring([Slot(sb(f"rAC{i}", [128, T], F32)) for i in range(6)])
    ringS = Ring([Slot(sb(f"rS{i}", [128, T], F32)) for i in range(6)])
    small = sb("small", [128, 256], F32)
    smb = {}

    def sm(name):
        if name not in smb:
            smb[name] = Buf()
        return smb[name]

    A_QK = 0
    A_VT = A_QK + 8 * T
    A_ZS = A_VT + NB * 8 * 128
    A_G = A_ZS + 8 * T // 2
    GW = 4 * 1024 + 8 * 512 + 3 * 1024 + 1024
    A_END = A_G + GW + 1024
    ARENA = max(A_END, NFC * T // 2, 16 * T, 6 * 2048 + 6 * 1024)
    arena = sb("arena", [128, ARENA], F32)
    qkT = arena[:, A_QK:A_QK + 8 * T].rearrange("p (c t) -> p c t", c=8)
    qkb = [Buf() for _ in range(8)]
    v_tok = arena[:, A_VT:A_VT + NB * 8 * 128].rearrange("p (b h d) -> p b h d", b=NB, h=8)
    vtb = [Buf() for _ in range(8)]
    zs = arena[:, A_ZS:A_ZS + 8 * T // 2].bitcast(BF16).rearrange("p (c t) -> p c t", c=8)
    zsb = [Buf() for _ in range(8)]
    o = A_G
    tmpG = Ring([Slot(arena[:, o + i * 1024:o + (i + 1) * 1024].rearrange("p (h d) -> p h d", h=8)) for i in range(4)])
    o += 4096
    Pbuf = [[Slot(arena[:, o + (hf * 2 + pp) * 512:o + (hf * 2 + pp + 1) * 512].rearrange("p (h d) -> p h d", h=4))
             for pp in range(2)] for hf in range(2)]
    o += 2048
    PTbuf = [[Slot(arena[:, o + (hf * 2 + pp) * 512:o + (hf * 2 + pp + 1) * 512].rearrange("p (h d) -> p h d", h=4))
              for pp in range(2)] for hf in range(2)]
    o += 2048
    TTt = Slot(arena[:, o:o + 1024].rearrange("p (h d) -> p h d", h=8)); o += 1024
    TTb = [Buf() for _ in range(8)]
    QKDT = Slot(arena[:, o:o + 1024].rearrange("p (h d) -> p h d", h=8)); o += 1024
    qgt = Slot(arena[:, o:o + 1024].rearrange("p (h d) -> p h d", h=8)); o += 1024
    kdec = Slot(arena[:, o:o + 1024].rearrange("p (h d) -> p h d", h=8)); o += 1024
    gcbd = Slot(arena[:, o:o + 1024]); o += 1024
    actf = arena[:, 0:NFC * T // 2].bitcast(BF16).rearrange("p (c t) -> p c t", c=NFC)
    actfb = [Buf() for _ in range(NFC)]
    yT = arena[:, 0:16 * T].rearrange("p (c t) -> p c t", c=16)
    yTb = [Buf() for _ in range(16)]
    NST = 6
    sin_ = Ring([Slot(arena[:, i * 2048:(i + 1) * 2048]) for i in range(NST)])
    sout = Ring([Slot(arena[:, NST * 2048 + i * 1024:NST * 2048 + (i + 1) * 1024].bitcast(BF16)) for i in range(NST)])

    def pst(i):
        t = es.enter_context(nc.psum_tensor(f"ps{i}", [128, 512], F32))
        return t.ap() if hasattr(t, "ap") else t[:]
    ps = [Slot(pst(i)) for i in range(8)]
    ring_mix = Ring(ps[0:3] + [ps[6]])
    PS = ps[3]
    PSring = Ring([ps[3], ps[7]])
    gring = Ring(ps[4:8])
    gring2 = Ring(ps[4:6])
    ring_ffn = Ring(ps[0:3] + ps[4:8])

    ident = cst[:, K_ID:K_ID + 128]
    tri = cst[:, K_TRI:K_TRI + 128]
    onesf = cst[:, K_ONE:K_ONE + 128]
    maskneg = cst[:, K_MNEG:K_MNEG + 128]
    strict = cst[:, K_STR:K_STR + 128]
    ind = cst[0:8, K_IND:K_IND + 1024]

    def E(eng_name, e):
        return e

    def mm(out_, lhsT, rhs, start, stop, r, w):
        P.op("pe", lambda e: e.matmul(out_, lhsT, rhs, start=start, stop=stop), r, w)

    def tr(out_, in_, r, w):
        P.op("pe", lambda e: e.transpose(out_, in_, ident), list(r) + [cstb], w)

    def act(out_, in_, func, r, w, bias=None, scale=None, accum=None):
        kw = {}
        if bias is not None:
            kw["bias"] = bias
        if scale is not None:
            kw["scale"] = scale
        if accum is not None:
            kw["accum_out"] = accum
        P.op("act", lambda e: e.activation(out_, in_, func, **kw), r, w)

    def tt(eng, out_, a, b, op, r, w):
        P.op(eng, lambda e: e.tensor_tensor(out_, a, b, op), r, w)

    def ts(eng, out_, a, s1, s2, op0, op1, r, w):
        if op1 is None:
            P.op(eng, lambda e: e.tensor_scalar(out_, a, s1, None, op0), r, w)
        else:
            P.op(eng, lambda e: e.tensor_scalar(out_, a, s1, s2, op0, op1), r, w)

    def stt(eng, out_, a, s, b, op0, op1, r, w):
        eng = "dve"
        P.op(eng, lambda e: e.scalar_tensor_tensor(out_, a, s, b, op0, op1), r, w)

    def cp(eng, out_, in_, r, w):
        if eng == "act":
            P.op("act", lambda e: e.copy(out_, in_), r, w)
        else:
            P.op(eng, lambda e: e.tensor_copy(out_, in_), r, w)

    def mset(eng, ap, val, w):
        P.op(eng, lambda e: e.memset(ap, val), (), w)

    def recip(out_, in_, r, w):
        P.op("dve", lambda e: e.reciprocal(out_, in_), r, w)

    def dma(q, out_, in_, r, w, ctr):
        return P.op(q, lambda e: e.dma_start(out=out_, in_=in_), r, w, dma=True, ctr=ctr)

    dma("sp", cst[:, :], consts[:, :], [], [cstb], cstb)
    for l in range(DEPTH):
        dma("sp", cvt[:, l, :], cvec[l, :, :], [], [cvtb], cvtb)
    cp("dve", onesb[:, :], onesf, [cstb], [sm("onesb")])
    for l in range(DEPTH):
        act(negA[:, l, :], cvt[:, l, O_ALOG:O_ALOG + 8], AF.Exp, [cvtb], [sm("negA")])
        ts("dve", negA[:, l, :], negA[:, l, :], -1.0, None, OP.mult, None, [sm("negA")], [sm("negA")])

    def cv0(l, col, n=1):
        return cvt[:, l, col:col + n]

    pieces = []
    order = []
    oth = [("q", c, 2048 + c * 128) for c in range(4)] + [("k", c, 2560 + c * 128) for c in range(4)] + \
          [("v", c, 3072 + c * 128) for c in range(8)] + [("z", c, 4096 + c * 128) for c in range(8)]
    for c in range(8):
        order += [("av", c, c * 128), ("ag", c, 1024 + c * 128)]
    for c in range(8):
        pass
    order2 = []
    for c in range(8):
        order2 += [("av", c, c * 128), ("ag", c, 1024 + c * 128), oth[2 * c], oth[2 * c + 1]]
    order2 += oth[16:24]
    order = order2
    for l in range(0 if NOW else DEPTH):
        for ci, (_, _, c0) in enumerate(order):
            pieces.append((w_in[l, :, c0:c0 + 128].rearrange("(k p) n -> p k n", p=128), (16, 128), win_s[l, ci]))
        pieces.append((w_in[l, :, 5120:5136].rearrange("(k p) n -> p k n", p=128), (16, 16), wba_s[l]))
        if WIN_ONLY:
            continue
        for m in range(8):
            pieces.append((w_pw[l, :, m * 128:(m + 1) * 128].rearrange("(k p) n -> p k n", p=128), (8, 128), wpw_s[l, m]))
        for m in range(16):
            pieces.append((w_out[l, :, m * 128:(m + 1) * 128].rearrange("(k p) n -> p k n", p=128), (16, 128), wout_s[l, m]))
        for c in range(NFC):
            for hf in range(2):
                c0 = hf * DFF + c * 128
                pieces.append((w_up[l, :, c0:c0 + 128].rearrange("(k p) n -> p k n", p=128), (16, 128), wup_s[l, 2 * c + hf]))
        for m in range(16):
            for q4 in range(4):
                pieces.append((w_dn[l, q4 * 1408:(q4 + 1) * 1408, m * 128:(m + 1) * 128].rearrange("(k p) n -> p k n", p=128),
                               (11, 128), wdn_s[l, m][:, q4 * 1408:(q4 + 1) * 1408]))
    stores = []
    ceng = ["act", "dve"]
    if cfg.get("STOP", 99) <= -1:
        pieces = pieces[:cfg.get("NPIECE", 0)]
    SUB = cfg.get("SUB", 99)
    for i, (src, (kk, nn), dst) in enumerate(pieces):
        n = kk * nn
        si = sin_.next()
        so = sout.next()
        dma("sp", si.ap[:, 0:n].rearrange("p (k n) -> p k n", k=kk), src, [], [si.buf], si.buf)
        cp(ceng[i % 2], so.ap[:, 0:n], si.ap[:, 0:n], [si.buf], [so.buf])
        stores.append(dma("pool", dst, so.ap[:, 0:n], [so.buf], [], so.buf))
    dstores = []
    hbufs = {}
    if not NOW:
        for l in range(DEPTH):
            for c in range(8):
                sl = wring.next()
                hb = hbufs.setdefault(id(sl), [Buf(), Buf()])
                for k in range(31):
                    if k % 2 == 0:
                        ts("dve", sl.ap[:, k * 128:(k + 1) * 128], ident, cv0(l, O_DWW + c * 31 + k), None,
                           OP.mult, None, [cstb, cvtb], [hb[0]])
                    else:
                        act(sl.ap[:, k * 128:(k + 1) * 128], ident, AF.Identity, [cstb, cvtb], [hb[1]],
                            scale=cv0(l, O_DWW + c * 31 + k))
                dstores.append(dma("sp", wdg_s[l, c], sl.ap[:, 0:31 * 128], [sl.buf, hb[0], hb[1]], [], sl.buf))
    P.barrier(extra=stores[-NST:] + dstores)

    def cv(l, col, n=1):
        return cvt[:, l, col:col + n]

    def rmsnorm(l, gcol, final=False):
        for kc in range(16):
            sq = tmpB.next()
            act(sq.ap, xT[:, kc, :], AF.Square, [xTb[kc]], [sq.buf])
            mm(PS.ap[:, 0:T], onesb[:, :], sq.ap, kc == 0, kc == 15, [sq.buf, sm("onesb")], [PS.buf])
        v = tmpF.next()
        ts("dve", v.ap, PS.ap[:, 0:T], 1.0 / D, EPS, OP.mult, OP.add, [PS.buf], [v.buf])
        act(v.ap, v.ap, AF.Sqrt, [v.buf], [v.buf])
        recip(v.ap, v.ap, [v.buf], [v.buf])
        for kc in range(16):
            eng = "dve" if kc % 2 == 0 else "pool"
            if final:
                stt(eng, yT[:, kc, :], xT[:, kc, :], cv(l, gcol + kc), v.ap, OP.mult, OP.mult,
                    [xTb[kc], v.buf, cvtb], [yTb[kc]])
            else:
                stt(eng, hT[:, kc, :], xT[:, kc, :], cv(l, gcol + kc), v.ap, OP.mult, OP.mult,
                    [xTb[kc], v.buf, cvtb], [hTb[kc]])

    def conv_small(l, first, bank_ap, bank_buf, hist_ap, hist_buf, ntap, wcol, bcol, eng="dve"):
        H = ntap - 1
        cb = cring.next()
        if first:
            mset("pool", cb.ap[:, 0:H], 0.0, [cb.buf])
        else:
            cp("pool", cb.ap[:, 0:H], hist_ap, [hist_buf], [cb.buf])
        cp("act", cb.ap[:, H:H + T], bank_ap, [bank_buf], [cb.buf])
        cp("pool", hist_ap, cb.ap[:, T:T + H], [cb.buf], [hist_buf])
        acc = tmpF.next()
        if bcol is None:
            ts(eng, acc.ap, cb.ap[:, 0:T], cv(l, wcol), None, OP.mult, None, [cb.buf, cvtb], [acc.buf])
        else:
            ts(eng, acc.ap, cb.ap[:, 0:T], cv(l, wcol), cv(l, bcol), OP.mult, OP.add, [cb.buf, cvtb], [acc.buf])
        for k in range(1, ntap):
            stt(eng, acc.ap, cb.ap[:, k:k + T], cv(l, wcol + k), acc.ap, OP.mult, OP.add, [cb.buf, acc.buf, cvtb], [acc.buf])
        return acc

    class Pipe:
        def __init__(self):
            self.q = []
            self.step = 0
            self.n = 0

        def defer(self, delay, fn):
            self.q.append((self.step + delay, self.n, fn))
            self.n += 1

        def tick(self):
            self.step += 1
            due = sorted([e for e in self.q if e[0] <= self.step])
            self.q = [e for e in self.q if e[0] > self.step]
            for _, _, f in due:
                f()

        def flush(self):
            while self.q:
                self.tick()

    pipe = Pipe()
    pshalf = [None]

    def st_conf(l, first, c, bank):
        sg = tmpF.next()
        act(sg.ap, bank.ap[:, T:2 * T], AF.Sigmoid, [bank.buf], [sg.buf])
        ub = ubring.take()
        hb = histub[l][c]
        if first:
            mset("pool", ub.ap[:, 0:30], 0.0, [ub.buf])
        else:
            cp("pool", ub.ap[:, 0:30], histu[:, l, c, :], [hb], [ub.buf])
        tt("dve", ub.ap[:, 30:30 + T], bank.ap[:, 0:T], sg.ap, OP.mult, [bank.buf, sg.buf], [ub.buf])
        cp("pool", histu[:, l, c, :], ub.ap[:, T:T + 30], [ub.buf], [hb])

        def d2():
            dsl = load_w(wdg_s[l, c])
            cbank = ring_mix.next()
            for k in range(31):
                mm(cbank.ap[:, 0:T], dsl.ap[:, k * 128:(k + 1) * 128], ub.ap[:, k:k + T], k == 0, k == 30, [dsl.buf, ub.buf], [cbank.buf])
            ub.free()
            aD = ringAD.take()
            act(aD.ap, cbank.ap[:, 0:T], AF.Identity, [cbank.buf, cvtb], [aD.buf], bias=cv(l, O_DWB + c))
            yb = tmpB.take()
            sqb = tmpB.take()
            cp("pool", yb.ap, aD.ap, [aD.buf], [yb.buf])
            act(sqb.ap, aD.ap, AF.Square, [aD.buf], [sqb.buf])

            def d3():
                psb = PSring.take()
                mm(psb.ap[:, 0:T], onesb[:, :], yb.ap, True, True, [yb.buf, sm("onesb")], [psb.buf])
                mm(psb.ap[:, T:2 * T], onesb[:, :], sqb.ap, True, True, [sqb.buf, sm("onesb")], [psb.buf])
                yb.free()
                sqb.free()

                def d4():
                    m2 = ringS.take()
                    act(m2.ap, psb.ap[:, 0:T], AF.Square, [psb.buf], [m2.buf], scale=1.0 / 128)
                    stt("dve", m2.ap, psb.ap[:, T:2 * T], 1.0 / 128, m2.ap, OP.mult, OP.subtract, [psb.buf, m2.buf], [m2.buf])
                    ts("dve", m2.ap, m2.ap, EPS, None, OP.add, None, [m2.buf], [m2.buf])
                    stt("dve", aD.ap, psb.ap[:, 0:T], -1.0 / 128, aD.ap, OP.mult, OP.add, [psb.buf, aD.buf], [aD.buf])
                    psb.free()

                    def d5():
                        act(m2.ap, m2.ap, AF.Sqrt, [m2.buf], [m2.buf])
                        recip(m2.ap, m2.ap, [m2.buf], [m2.buf])
                        tt("pool", aD.ap, aD.ap, m2.ap, OP.mult, [aD.buf, m2.buf], [aD.buf])
                        m2.free()

                        def d6():
                            act(actA[:, c, :], aD.ap, AF.Silu, [aD.buf, cvtb], [actAb[c]], scale=cv(l, O_LNG + c), bias=cv(l, O_LNB + c))
                            aD.free()
                        pipe.defer(1, d6)
                    pipe.defer(1, d5)
                pipe.defer(1, d4)
            pipe.defer(1, d3)
        pipe.defer(2, d2)

    def st_qkv(l, first, kind, c, bank):
        ch = {"q": c, "k": 4 + c, "v": 8 + c}[kind]
        hist_ap, hist_buf = histq[:, l, ch, :], histqb[l][ch]
        cb = cring.take()
        if first:
            mset("pool", cb.ap[:, 0:3], 0.0, [cb.buf])
        else:
            cp("pool", cb.ap[:, 0:3], hist_ap, [hist_buf], [cb.buf])
        cp("act", cb.ap[:, 3:3 + T], bank.ap[:, 0:T], [bank.buf], [cb.buf])
        cp("pool", hist_ap, cb.ap[:, T:T + 3], [cb.buf], [hist_buf])

        def d1():
            wcol = O_GCW + ch * 4
            acc = ringACC.take()
            ts("dve", acc.ap, cb.ap[:, 0:T], cv(l, wcol), None, OP.mult, None, [cb.buf, cvtb], [acc.buf])
            for k in range(1, 4):
                stt("dve", acc.ap, cb.ap[:, k:k + T], cv(l, wcol + k), acc.ap, OP.mult, OP.add, [cb.buf, acc.buf, cvtb], [acc.buf])
            act(acc.ap, acc.ap, AF.Silu, [acc.buf], [acc.buf])
            cb.free()
            if kind == "v":
                def d2v():
                    gb = gring2.take()
                    for b in range(NB):
                        tr(gb.ap[:, b * 128:(b + 1) * 128], acc.ap[:, b * 128:(b + 1) * 128], [acc.buf], [gb.buf])
                    acc.free()

                    def d3v():
                        cp("act", v_tok[:, :, c, :], gb.ap[:, 0:NB * 128].rearrange("p (b d) -> p b d", b=NB), [gb.buf], [vtb[c]])
                        gb.free()
                    pipe.defer(1, d3v)
                pipe.defer(1, d2v)
                return
            sqb = tmpB.take()
            act(sqb.ap, acc.ap, AF.Square, [acc.buf], [sqb.buf])

            def d2():
                if pshalf[0] is None:
                    psb = PSring.take(2)
                    pshalf[0] = psb
                    po = 0
                else:
                    psb = pshalf[0]
                    pshalf[0] = None
                    po = T
                mm(psb.ap[:, po:po + T], onesb[:, :], sqb.ap, True, True, [sqb.buf, sm("onesb")], [psb.buf])
                sqb.free()

                def d3():
                    rs = ringS.take()
                    ts("dve", rs.ap, psb.ap[:, po:po + T], EPS, None, OP.add, None, [psb.buf], [rs.buf])
                    psb.free()
                    act(rs.ap, rs.ap, AF.Sqrt, [rs.buf], [rs.buf])

                    def d4():
                        recip(rs.ap, rs.ap, [rs.buf], [rs.buf])
                        if kind == "q":
                            stt("dve", qkT[:, ch, :], acc.ap, 128.0 ** -0.5, rs.ap, OP.mult, OP.mult, [acc.buf, rs.buf], [qkb[ch]])
                        else:
                            tt("dve", qkT[:, ch, :], acc.ap, rs.ap, OP.mult, [acc.buf, rs.buf], [qkb[ch]])
                        rs.free()
                        acc.free()
                    pipe.defer(1, d4)
                pipe.defer(1, d3)
            pipe.defer(1, d2)
        pipe.defer(1, d1)

    def post_z(l, c, bank):
        act(zs[:, c, :], bank.ap[:, 0:T], AF.Silu, [bank.buf], [zsb[c]])

    SM_B, SM_NB, SM_G = 0, 8 * NB, 16 * NB
    SM_GG = 24 * NB
    SM_EGC = SM_GG + 16
    SM_NEGC = SM_EGC + 8
    SM_EGL = SM_NEGC + 8
    SM_EDEC = SM_EGL + 8
    SM_SSQ = SM_EDEC + 8
    SM_X = SM_SSQ + 8
    gcTs = sb("gcTs", [8, 128], F32)

    def ba_proj(l):
        wb = wbar.next()
        dma("sp", wb.ap, wba_s[l], [], [wb.buf], wb.buf)
        gb = gring.next()
        wv = wb.ap.rearrange("p (k n) -> p k n", k=16)
        for b in range(NB):
            for kc in range(16):
                mm(gb.ap[:, b * 16:(b + 1) * 16], hT[:, kc, b * 128:(b + 1) * 128], wv[:, kc, :], kc == 0, kc == 15,
                   [hTb[kc], wb.buf], [gb.buf])
        for b in range(NB):
            bsl = small[:, SM_B + b * 8:SM_B + (b + 1) * 8]
            act(bsl, gb.ap[:, b * 16:b * 16 + 8], AF.Sigmoid, [gb.buf], [sm("btok")])
            ts("pool", small[:, SM_NB + b * 8:SM_NB + (b + 1) * 8], bsl, -1.0, None, OP.mult, None, [sm("btok")], [sm("nbtok")])
            xx = small[:, SM_X + b * 8:SM_X + (b + 1) * 8]
            tt("dve", xx, gb.ap[:, b * 16 + 8:b * 16 + 16], cv(l, O_DTB, 8), OP.add, [gb.buf, cvtb], [sm("xx")])
            act(xx, xx, AF.Exp, [sm("xx")], [sm("xx")])
            act(xx, xx, AF.Ln, [sm("xx")], [sm("xx")], bias=1.0)
            tt("dve", small[:, SM_G + b * 8:SM_G + (b + 1) * 8], xx, negA[:, l, :], OP.mult, [sm("xx"), sm("negA")], [sm("gtok")])

    def gdn_prep(l, b, filler=None):
        bs = slice(b * 128, (b + 1) * 128)
        gsl = small[:, SM_G + b * 8:SM_G + (b + 1) * 8]
        X1 = gring.next()
        mm(X1.ap[:, 0:8], tri, gsl, True, True, [cstb, sm("gtok")], [X1.buf])
        mm(X1.ap[:, 8:16], onesf, gsl, True, True, [cstb, sm("gtok")], [X1.buf])
        mm(X1.ap[0:8, 16:144], gsl, tri, True, True, [cstb, sm("gtok")], [X1.buf])
        gg = small[:, SM_GG:SM_GG + 16]
        cp("act", gg, X1.ap[:, 0:16], [X1.buf], [sm("gg")])
        cp("dve", gcTs[:, :], X1.ap[0:8, 16:144], [X1.buf], [sm("gcTs")])
        if filler is not None:
            filler()
        if SUB <= 0.2:
            return
        act(small[:, SM_EGC:SM_EGC + 8], gg[:, 0:8], AF.Exp, [sm("gg")], [sm("egc")])
        ts("pool", small[:, SM_NEGC:SM_NEGC + 8], small[:, SM_EGC:SM_EGC + 8], -1.0, None, OP.mult, None, [sm("egc")], [sm("negc")])
        act(small[:, SM_EGL:SM_EGL + 8], gg[:, 8:16], AF.Exp, [sm("gg")], [sm("egl")])
        tt("dve", small[:, SM_EDEC:SM_EDEC + 8], gg[:, 8:16], gg[:, 0:8], OP.subtract, [sm("gg")], [sm("edec")])
        act(small[:, SM_EDEC:SM_EDEC + 8], small[:, SM_EDEC:SM_EDEC + 8], AF.Exp, [sm("edec")], [sm("edec")])
        if SUB <= 0.4:
            return
        gv = gcbd.ap[0:8, :].rearrange("p (h d) -> p h d", h=8)
        indv = ind.rearrange("p (h d) -> p h d", h=8)
        for h in range(8):
            tt("pool", gv[:, h, :], indv[:, h, :], gcTs[:, :], OP.mult, [cstb, sm("gcTs")], [gcbd.buf])
        if SUB <= 0.6:
            return
        G2 = [gring.next(), gring.next()]
        for hf in range(2):
            mm(G2[hf].ap[:, :], onesf[0:8, :], gcbd.ap[0:8, hf * 512:(hf + 1) * 512], True, True, [cstb, gcbd.buf], [G2[hf].buf])
        if SUB <= 0.8:
            return
        EG = tmpG.next()
        E1 = tmpG.next()
        for hf in range(2):
            act(EG.ap[:, hf * 4:(hf + 1) * 4, :], G2[hf].ap.rearrange("p (h d) -> p h d", h=4), AF.Exp, [G2[hf].buf], [EG.buf])
        if SUB <= 0.85:
            return
        ngc = small[:, SM_X + 16:SM_X + 24]
        ts("pool", ngc, gg[:, 0:8], -1.0, None, OP.mult, None, [sm("gg")], [sm("ngc")])
        for hf in range(2):
            act(E1.ap[:, hf * 4:(hf + 1) * 4, :], G2[hf].ap.rearrange("p (h d) -> p h d", h=4), AF.Identity, [G2[hf].buf], [E1.buf])
        for h in range(8):
            stt("dve", E1.ap[:, h, :], E1.ap[:, h, :], ngc[:, h:h + 1], maskneg,
                OP.add, OP.add, [E1.buf, sm("ngc"), cstb], [E1.buf])
        if SUB <= 0.9:
            return
        act(E1.ap, E1.ap, AF.Exp, [E1.buf], [E1.buf])
        if SUB <= 1:
            return
        for h in range(8):
            tt("pool", qgt.ap[:, h, :], qkT[:, h // 2, bs], EG.ap[:, h, :], OP.mult, [qkb[h // 2], EG.buf], [qgt.buf])
        Y = [gring.next(), gring.next()]
        for hq in range(4):
            kb = qkT[:, 4 + hq, bs]
            qb = qkT[:, hq, bs]
            yo = (hq % 2) * 256
            mm(Y[hq // 2].ap[:, yo:yo + 128], kb, kb, True, True, [qkb[4 + hq]], [Y[hq // 2].buf])
            mm(Y[hq // 2].ap[:, yo + 128:yo + 256], kb, qb, True, True, [qkb[4 + hq], qkb[hq]], [Y[hq // 2].buf])
        for h in range(8):
            hq = h // 2
            yo = (hq % 2) * 256
            tt("dve", QKDT.ap[:, h, :], Y[hq // 2].ap[:, yo + 128:yo + 256], E1.ap[:, h, :], OP.mult, [Y[hq // 2].buf, E1.buf], [QKDT.buf])
        for h in range(8):
            tt("pool", E1.ap[:, h, :], E1.ap[:, h, :], strict, OP.mult, [E1.buf, cstb], [E1.buf])
        for h in range(8):
            hq = h // 2
            yo = (hq % 2) * 256
            pt0 = PTbuf[h // 4][0]
            stt("dve", pt0.ap[:, h % 4, :], Y[hq // 2].ap[:, yo:yo + 128], small[:, SM_NB + b * 8 + h:SM_NB + b * 8 + h + 1],
                E1.ap[:, h, :], OP.mult, OP.mult, [Y[hq // 2].buf, sm("nbtok"), E1.buf], [pt0.buf])
        for h in range(8):
            tt("pool", TTt.ap[:, h, :], PTbuf[h // 4][0].ap[:, h % 4, :], ident, OP.add, [PTbuf[h // 4][0].buf, cstb], [TTb[h]])
        if SUB <= 2:
            return
        Kt = gring.next()
        for hq in range(4):
            tr(Kt.ap[:, hq * 128:(hq + 1) * 128], qkT[:, 4 + hq, bs], [qkb[4 + hq]], [Kt.buf])
        for h in range(8):
            act(kdec.ap[:, h, :], Kt.ap[:, (h // 2) * 128:(h // 2 + 1) * 128], AF.Identity, [Kt.buf, sm("edec")], [kdec.buf],
                scale=small[:, SM_EDEC + h:SM_EDEC + h + 1])
        if SUB <= 3:
            return
        for hf in range(2):
            Pb = gring.next()
            for j in range(4):
                tr(Pb.ap[:, j * 128:(j + 1) * 128], PTbuf[hf][0].ap[:, j, :], [PTbuf[hf][0].buf], [Pb.buf])
            cp("act", Pbuf[hf][0].ap, Pb.ap.rearrange("p (h d) -> p h d", h=4), [Pb.buf], [Pbuf[hf][0].buf])
        if SUB <= 4:
            return
        cur = [0, 0]
        for k in range(1, 7):
            for hf in range(2):
                c_ = cur[hf]
                n_ = 1 - c_
                Pc, PTc, Pn, PTn = Pbuf[hf][c_], PTbuf[hf][c_], Pbuf[hf][n_], PTbuf[hf][n_]
                A = gring.next()
                for j in range(4):
                    mm(A.ap[:, j * 128:(j + 1) * 128], PTc.ap[:, j, :], Pc.ap[:, j, :], True, True, [PTc.buf, Pc.buf], [A.buf])
                cp("act", Pn.ap, A.ap.rearrange("p (h d) -> p h d", h=4), [A.buf], [Pn.buf])
                if k < 6:
                    B = gring.next()
                    for j in range(4):
                        mm(B.ap[:, j * 128:(j + 1) * 128], Pc.ap[:, j, :], PTc.ap[:, j, :], True, True, [PTc.buf, Pc.buf], [B.buf])
                    cp("dve", PTn.ap, B.ap.rearrange("p (h d) -> p h d", h=4), [B.buf], [PTn.buf])
                C = gring.next()
                for j in range(4):
                    h = hf * 4 + j
                    mm(C.ap[:, j * 128:(j + 1) * 128], Pn.ap[:, j, :], TTt.ap[:, h, :], True, True, [Pn.buf, TTb[h]], [C.buf])
                for j in range(4):
                    h = hf * 4 + j
                    tt("dve", TTt.ap[:, h, :], TTt.ap[:, h, :], C.ap[:, j * 128:(j + 1) * 128], OP.add, [TTb[h], C.buf], [TTb[h]])
                cur[hf] = n_

    def gdn_chain(l, b):
        bs = slice(b * 128, (b + 1) * 128)
        if SUB <= 5:
            return

        def pair():
            return [gring.next(), gring.next()]

        def pv(banks, h):
            return banks[h // 4].ap[:, (h % 4) * 128:(h % 4 + 1) * 128], banks[h // 4].buf

        KS = pair()
        for h in range(8):
            o_, ob = pv(KS, h)
            mm(o_, qkT[:, 4 + h // 2, bs], Sst[:, l, h, :], True, True, [qkb[4 + h // 2], Sb[l][h]], [ob])
        R = tmpG.next()
        for h in range(8):
            o_, ob = pv(KS, h)
            stt("dve", R.ap[:, h, :], o_, small[:, SM_NEGC + h:SM_NEGC + h + 1], v_tok[:, b, h, :], OP.mult, OP.add,
                [ob, sm("negc"), vtb[h]], [R.buf])
        VN = pair()
        for h in range(8):
            o_, ob = pv(VN, h)
            mm(o_, TTt.ap[:, h, :], R.ap[:, h, :], True, True, [TTb[h], R.buf], [ob])
        vn = tmpG.next()
        for h in range(8):
            o_, ob = pv(VN, h)
            act(vn.ap[:, h, :], o_, AF.Identity, [ob, sm("btok")], [vn.buf], scale=small[:, SM_B + b * 8 + h:SM_B + b * 8 + h + 1])
        if SUB <= 6:
            return
        O = pair()
        for h in range(8):
            o_, ob = pv(O, h)
            mm(o_, qgt.ap[:, h, :], Sst[:, l, h, :], True, False, [qgt.buf, Sb[l][h]], [ob])
            mm(o_, QKDT.ap[:, h, :], vn.ap[:, h, :], False, True, [QKDT.buf, vn.buf], [ob])
        SU = pair()
        for h in range(8):
            o_, ob = pv(SU, h)
            mm(o_, kdec.ap[:, h, :], vn.ap[:, h, :], True, True, [kdec.buf, vn.buf], [ob])
        for h in range(8):
            o_, ob = pv(SU, h)
            stt("dve", Sst[:, l, h, :], Sst[:, l, h, :], small[:, SM_EGL + h:SM_EGL + h + 1], o_, OP.mult, OP.add,
                [Sb[l][h], sm("egl"), ob], [Sb[l][h]])
        if SUB <= 7:
            return
        on = tmpG.next()
        junk = tmpF.next()
        ssq = small[:, SM_SSQ:SM_SSQ + 8]
        for h in range(8):
            o_, ob = pv(O, h)
            act(junk.ap[:, 0:128], o_, AF.Square, [ob], [junk.buf, sm("ssq")], accum=ssq[:, h:h + 1])
        ts("dve", ssq, ssq, 1.0 / 128, EPS, OP.mult, OP.add, [sm("ssq")], [sm("ssq")])
        act(ssq, ssq, AF.Sqrt, [sm("ssq")], [sm("ssq")])
        recip(ssq, ssq, [sm("ssq")], [sm("ssq")])
        for h in range(8):
            o_, ob = pv(O, h)
            act(on.ap[:, h, :], o_, AF.Identity, [ob, sm("ssq")], [on.buf], scale=ssq[:, h:h + 1])
        if SUB <= 8:
            return
        OT = pair()
        for h in range(8):
            o_, ob = pv(OT, h)
            tr(o_, on.ap[:, h, :], [on.buf], [ob])
        for h in range(8):
            o_, ob = pv(OT, h)
            stt("dve", mixcat[:, 8 + h, bs], o_, cv(l, O_GNG), zs[:, h, bs], OP.mult, OP.mult, [ob, cvtb, zsb[h]], [mixb[8 + h]])

    def load_w(src, c=None):
        sl = wring.next()
        if c is None:
            dst = sl.ap[:, 0:src.shape[-1]]
        else:
            dst = sl.ap.rearrange("p (c f) -> p c f", c=c)
        dma("sp", dst, src, [], [sl.buf], sl.buf)
        return sl

    STOP = cfg.get("STOP", 99)

    def unit(seq, tile, l):
        first = tile == 0
        t0 = tile * T
        if STOP <= 0:
            return
        if l == 0:
            for b in range(NB):
                sl = wring.next()
                sf = sl.ap.bitcast(F32)
                dma("sp", sf, x[seq, t0 + b * 128:t0 + (b + 1) * 128, :], [], [sl.buf], sl.buf)
                for g in range(4):
                    gb = gring.next()
                    for j in range(4):
                        kc = g * 4 + j
                        tr(gb.ap[:, j * 128:(j + 1) * 128], sf[:, kc * 128:(kc + 1) * 128], [sl.buf], [gb.buf])
                    cp("act" if g % 2 == 0 else "dve", xT[:, g * 4:(g + 1) * 4, b * 128:(b + 1) * 128],
                       gb.ap.rearrange("p (j t) -> p j t", j=4), [gb.buf], [xTb[g * 4 + j] for j in range(4)])
        if first:
            for h in range(8):
                mset("pool", Sst[:, l, h, :], 0.0, [Sb[l][h]])
        if STOP <= 1:
            return
        rmsnorm(l, O_MIXG)
        if STOP <= 2:
            return
        ba_proj(l)
        if STOP <= 3:
            return
        for g in range(20):
            pipe.tick()
            sl = load_w(win_s[l, 2 * g:2 * g + 2].rearrange("c p f -> p c f"), 2)
            wv = sl.ap.rearrange("p (c k n) -> p c k n", c=2, k=16)
            kinds = order[2 * g:2 * g + 2]
            if kinds[0][0] == "av":
                bank = ring_mix.next()
                for j in range(2):
                    for kc in range(16):
                        mm(bank.ap[:, j * T:(j + 1) * T], wv[:, j, kc, :], hT[:, kc, :], kc == 0, kc == 15, [sl.buf, hTb[kc]], [bank.buf])
                st_conf(l, first, kinds[0][1], bank)
            else:
                for j in range(2):
                    bank = ring_mix.next()
                    for kc in range(16):
                        mm(bank.ap[:, 0:T], wv[:, j, kc, :], hT[:, kc, :], kc == 0, kc == 15, [sl.buf, hTb[kc]], [bank.buf])
                    kind, c, _ = kinds[j]
                    if kind in ("q", "k", "v"):
                        st_qkv(l, first, kind, c, bank)
                    else:
                        post_z(l, c, bank)
        pipe.flush()
        if STOP <= 4:
            return
        def pw_proj():
            for g in range(2):
                sl = load_w(wpw_s[l, 4 * g:4 * g + 4].rearrange("c p f -> p c f"), 4)
                wv = sl.ap.rearrange("p (c k n) -> p c k n", c=4, k=8)
                for j in range(4):
                    m = 4 * g + j
                    bank = ring_mix.next()
                    for kc in range(8):
                        mm(bank.ap[:, 0:T], wv[:, j, kc, :], actA[:, kc, :], kc == 0, kc == 7, [sl.buf, actAb[kc]], [bank.buf])
                    act(mixcat[:, m, :], bank.ap[:, 0:T], AF.Identity, [bank.buf, cvtb], [mixb[m]], bias=cv(l, O_PWB + m))
        if STOP <= 5:
            return
        for b in range(NB):
            gdn_prep(l, b, filler=pw_proj if b == 0 else None)
            gdn_chain(l, b)
        if STOP <= 6:
            return
        for g in range(8):
            sl = load_w(wout_s[l, 2 * g:2 * g + 2].rearrange("c p f -> p c f"), 2)
            wv = sl.ap.rearrange("p (c k n) -> p c k n", c=2, k=16)
            for j in range(2):
                m = 2 * g + j
                bank = ring_mix.next()
                for kc in range(16):
                    mm(bank.ap[:, 0:T], wv[:, j, kc, :], mixcat[:, kc, :], kc == 0, kc == 15, [sl.buf, mixb[kc]], [bank.buf])
                tt("dve", xT[:, m, :], xT[:, m, :], bank.ap[:, 0:T], OP.add, [xTb[m], bank.buf], [xTb[m]])
        P.barrier(sp=False)
        if STOP <= 7:
            return
        rmsnorm(l, O_FFNG)
        for c in range(NFC):
            sl = load_w(wup_s[l, 2 * c:2 * c + 2].rearrange("c p f -> p c f"), 2)
            wv = sl.ap.rearrange("p (c k n) -> p c k n", c=2, k=16)
            bank = ring_ffn.next()
            for j in range(2):
                for kc in range(16):
                    mm(bank.ap[:, j * T:(j + 1) * T], wv[:, j, kc, :], hT[:, kc, :], kc == 0, kc == 15, [sl.buf, hTb[kc]], [bank.buf])
            eng = "dve" if c % 2 == 0 else "pool"
            acc = conv_small(l, first, bank.ap[:, 0:T], bank.buf, histf[:, l, c, :], histfb[l][c], 3, O_FCW + c * 3, O_FCB + c, eng=eng)
            act(acc.ap, acc.ap, AF.Silu, [acc.buf], [acc.buf])
            tt("dve", actf[:, c, :], acc.ap, bank.ap[:, T:2 * T], OP.mult, [acc.buf, bank.buf], [actfb[c]])
        for m in range(16):
            bank = ring_ffn.next()
            for hf in range(2):
                sl = load_w(wdn_s[l, m][:, hf * 2816:(hf + 1) * 2816])
                wv = sl.ap[:, 0:2816].rearrange("p (k n) -> p k n", k=22)
                for kk in range(22):
                    kc = hf * 22 + kk
                    mm(bank.ap[:, 0:T], wv[:, kk, :], actf[:, kc, :], kc == 0, kc == NFC - 1, [sl.buf, actfb[kc]], [bank.buf])
            tt("dve", xT[:, m, :], xT[:, m, :], bank.ap[:, 0:T], OP.add, [xTb[m], bank.buf], [xTb[m]])
        P.barrier(sp=False)
        if l == DEPTH - 1:
            rmsnorm(l, O_FING, final=True)
            for b in range(NB):
                sl = wring.next()
                sf = sl.ap.bitcast(F32)
                for g in range(4):
                    gb = gring.next()
                    for j in range(4):
                        kc = g * 4 + j
                        tr(gb.ap[:, j * 128:(j + 1) * 128], yT[:, kc, b * 128:(b + 1) * 128], [yTb[kc]], [gb.buf])
                    cp("act" if g % 2 == 0 else "dve", sf[:, g * 512:(g + 1) * 512], gb.ap, [gb.buf], [sl.buf])
                dma("act", out[seq, t0 + b * 128:t0 + (b + 1) * 128, :], sf, [sl.buf], [], sl.buf)
            P.barrier(sp=False)

    for seq in range(NSEQ):
        for tile in range(NT):
            for l in range(DEPTH):
                unit(seq, tile, l)
    final_waits = []
    for sl in wring.items:
        for d in sl.buf.r.values():
            if d.dma:
                final_waits.append(d)
        if sl.buf.w is not None and sl.buf.w.dma:
            final_waits.append(sl.buf.w)
    P.barrier(extra=final_waits)

    nsig = {e: sum(1 for i in P.q[e] if i.signal and not i.dma) for e in ENGS}
    esems = {e: [es.enter_context(nc.semaphore(f"s_{e}_{k}")) for k in range(nsig[e] // EPOCH + 1)] for e in ENGS}
    nds = 0
    for i in P.dmas:
        b = i.ctr
        if b.ctr is None:
            b.ctr = [es.enter_context(nc.semaphore(f"dsem{nds}")), 0]
            nds += 1
        b.ctr[1] += 16
        i.sem, i.val = b.ctr[0], b.ctr[1]
    for e in ENGS:
        cnt = 0
        for i in P.q[e]:
            if i.dma:
                continue
            if i.signal:
                i.sem, i.val = esems[e][cnt // EPOCH], cnt % EPOCH + 1
                cnt += 1

    DUMP = cfg.get("DUMP", False)

    def emit(eng_name, eng):
        for i in P.q[eng_name]:
            if DUMP:
                print(eng_name, i.idx, "waits", [(d.eng, d.idx, str(d.sem), d.val, d.dma) for d in i.waits],
                      "fn" if i.fn else "barrier", "dma" if i.dma else "", "sig" if i.signal else "", str(i.sem), i.val)
            for d in i.waits:
                eng.wait_ge(d.sem, d.val)
            if i.fn is None:
                continue
            ins = i.fn(eng)
            if i.dma:
                ins.then_inc(i.sem, 16)
            elif i.signal:
                ins.then_inc(i.sem, 1)

    with nc.Block() as block:
        @block.sync
        def _(e):
            emit("sp", e)

        @block.scalar
        def _(e):
            emit("act", e)

        @block.vector
        def _(e):
            emit("dve", e)

        @block.gpsimd
        def _(e):
            emit("pool", e)

        @block.tensor
        def _(e):
            emit("pe", e)
    es.close()
    return nc


def host_consts():
    c = np.zeros((128, NK), np.float32)
    j = np.arange(128)[:, None]
    i = np.arange(128)[None, :]
    c[:, K_ID:K_ID + 128] = (i == j)
    c[:, K_TRI:K_TRI + 128] = (j <= i)
    c[:, K_ONE:K_ONE + 128] = 1.0
    c[:, K_MNEG:K_MNEG + 128] = np.where(i >= j, 0.0, -30000.0)
    c[:, K_STR:K_STR + 128] = (i > j)
    for h in range(8):
        c[h, K_IND + h * 128:K_IND + (h + 1) * 128] = 1.0
    return c


def host_cvec(inp, depth):
    cv = np.zeros((depth, 128, NCV), np.float32)
    for l in range(depth):
        def fm(v, n):
            return np.ascontiguousarray(np.asarray(v, np.float32).reshape(n, 128).T)
        cv[l, :, O_MIXG:O_MIXG + 16] = fm(inp["mix_norm_g"][l], 16)
        cv[l, :, O_FFNG:O_FFNG + 16] = fm(inp["ffn_norm_g"][l], 16)
        cv[l, :, O_FING:O_FING + 16] = fm(inp["final_norm_g"], 16)
        cv[l, :, O_DWW:O_DWW + 248] = np.asarray(inp["conv_dw_w"][l], np.float32).reshape(31, 8, 128).transpose(2, 1, 0).reshape(128, 248)
        cv[l, :, O_DWB:O_DWB + 8] = fm(inp["conv_dw_b"][l], 8)
        cv[l, :, O_LNG:O_LNG + 8] = fm(inp["conv_ln_g"][l], 8)
        cv[l, :, O_LNB:O_LNB + 8] = fm(inp["conv_ln_b"][l], 8)
        cv[l, :, O_PWB:O_PWB + 8] = fm(inp["conv_pw_b"][l], 8)
        cv[l, :, O_GCW:O_GCW + 64] = np.asarray(inp["gdn_conv_w"][l], np.float32).reshape(4, 16, 128).transpose(2, 1, 0).reshape(128, 64)
        cv[l, :, O_ALOG:O_ALOG + 8] = np.asarray(inp["gdn_a_log"][l], np.float32)[None, :]
        cv[l, :, O_DTB:O_DTB + 8] = np.asarray(inp["gdn_dt_bias"][l], np.float32)[None, :]
        cv[l, :, O_GNG] = np.asarray(inp["gdn_norm_g"][l], np.float32)
        cv[l, :, O_FCW:O_FCW + 132] = np.asarray(inp["ffn_conv_w"][l], np.float32).reshape(3, NFC, 128).transpose(2, 1, 0).reshape(128, 132)
        cv[l, :, O_FCB:O_FCB + NFC] = fm(inp["ffn_conv_b"][l], NFC)
    return cv


def run(inp, cfg, ncores):
    depth = cfg["DEPTH"]
    nc = build(cfg)
    cst = host_consts()
    cvv = host_cvec(inp, depth)
    f = lambda k: np.ascontiguousarray(np.asarray(inp[k], np.float32)[:depth])
    shared = dict(w_in=f("w_in"), w_pw=f("conv_pw_w"), w_out=f("w_out"), w_up=f("w_up"), w_dn=f("w_down"),
                  cvec=cvv, consts=cst)
    if cfg.get("STOP", 99) <= -2:
        shared = dict(cvec=cvv, consts=cst)
    if cfg.get("WSET", "all") == "in":
        shared = dict(w_in=f("w_in"), cvec=cvv, consts=cst)
    xs = np.asarray(inp["x"], np.float32)
    ns = cfg["NSEQ"]
    in_maps = []
    for c in range(ncores):
        m = dict(shared)
        m["x"] = np.ascontiguousarray(xs[c * ns:(c + 1) * ns])
        in_maps.append(m)
    res = run_bass_kernel_spmd(nc, in_maps, core_ids=list(range(ncores)))
    return np.concatenate([np.asarray(r["out"]) for r in res.results], axis=0)


def kernel(**inputs):
    return run(inputs, CFG, NCORES).astype(np.float32)
```

```python
import numpy as np
from contextlib import ExitStack
import concourse.bass as bass
import concourse.mybir as mybir
from concourse.bass_utils import run_bass_kernel_spmd

F32 = mybir.dt.float32
BF16 = mybir.dt.bfloat16
AF = mybir.ActivationFunctionType
OP = mybir.AluOpType

D = 2048
CC = 1024
INW = 5136
DFF = 5632
NFC = 44
EPS = 1e-6
NCORES = 8
CFG = dict(NSEQ=2, S=2048, T=256, DEPTH=2)
SAME_ENGINE_SYNC = True
EPOCH = 30000
KD = 19

O_MIXG, O_FFNG, O_FING = 0, 16, 32
O_DWW = 48
O_DWB = O_DWW + 248
O_LNG = O_DWB + 8
O_LNB = O_LNG + 8
O_PWB = O_LNB + 8
O_GCW = O_PWB + 8
O_ALOG = O_GCW + 64
O_DTB = O_ALOG + 8
O_GNG = O_DTB + 8
O_FCW = O_GNG + 1
O_FCB = O_FCW + 132
NCV = O_FCB + 44
K_ID, K_TRI, K_ONE, K_MNEG, K_STR, K_IND = 0, 128, 256, 384, 512, 640
NK = 640 + 1024


class Buf:
    __slots__ = ("w", "r", "ctr")

    def __init__(self):
        self.w = None
        self.r = {}
        self.ctr = None


class Inst:
    __slots__ = ("eng", "fn", "waits", "signal", "dma", "idx", "sem", "val", "ctr")

    def __init__(self, eng, fn, dma):
        self.eng, self.fn, self.dma = eng, fn, dma
        self.waits = []
        self.signal = False
        self.sem = None
        self.val = None
        self.ctr = None


ENGS = ("pe", "act", "dve", "pool", "sp")


class Prog:
    def __init__(self):
        self.q = {e: [] for e in ENGS}
        self.known = {e: {} for e in ENGS}
        self.kdma = {e: set() for e in ENGS}
        self.dmas = []

    def _wait(self, inst, d):
        if d is inst or d is None:
            return
        e = inst.eng
        if d.dma:
            if id(d) in self.kdma[e]:
                return
            self.kdma[e].add(id(d))
        else:
            if d.eng == e and (e == "pe" or not SAME_ENGINE_SYNC):
                return
            if self.known[e].get(d.eng, -1) >= d.idx:
                return
            self.known[e][d.eng] = d.idx
            d.signal = True
        inst.waits.append(d)

    def op(self, eng, fn, r=(), w=(), dma=False, ctr=None):
        inst = Inst(eng, fn, dma)
        inst.idx = len(self.q[eng])
        inst.ctr = ctr
        if dma:
            self.dmas.append(inst)
        for b in r:
            self._wait(inst, b.w)
        for b in w:
            self._wait(inst, b.w)
            for d in b.r.values():
                self._wait(inst, d)
        key = ("d", id(inst)) if dma else eng
        for b in r:
            b.r[key] = inst
        for b in w:
            b.w = inst
            b.r = {}
        self.q[eng].append(inst)
        return inst

    def barrier(self, extra=(), sp=True):
        last = {}
        for e in ENGS:
            last[e] = None
            for i in reversed(self.q[e]):
                if i.fn is not None:
                    last[e] = i
                    break
        for e in ENGS:
            if e == "sp" and not sp:
                continue
            inst = Inst(e, None, False)
            inst.idx = len(self.q[e])
            for e2 in ("pe", "act", "dve", "pool"):
                d = last[e2]
                if d is not None and e2 != e and not d.dma:
                    self._wait(inst, d)
            for d in extra:
                self._wait(inst, d)
            self.q[e].append(inst)


class Ring:
    def __init__(self, items):
        self.items = items
        self.i = 0

    def next(self):
        it = self.items[self.i % len(self.items)]
        self.i += 1
        return it

    def take(self, n=1):
        it = self.next()
        assert it.busy == 0, "ring slot re-allocated while a deferred user is pending"
        it.busy = n
        return it


class Slot:
    __slots__ = ("ap", "buf", "busy")

    def __init__(self, ap):
        self.ap = ap
        self.buf = Buf()
        self.busy = 0

    def free(self):
        assert self.busy > 0
        self.busy -= 1


def build(cfg):
    NSEQ, S, T, DEPTH = cfg["NSEQ"], cfg["S"], cfg["T"], cfg["DEPTH"]
    NB = T // 128
    NT = S // T
    nc = bass.Bass("TRN2", target_bir_lowering=False)
    P = Prog()
    es = ExitStack()

    def dram(name, shape, dt, kind):
        return nc.dram_tensor(name, list(shape), dt, kind=kind).ap()

    x = dram("x", [NSEQ, S, D], F32, "ExternalInput")
    NOW = cfg.get("STOP", 99) <= -2
    WIN_ONLY = cfg.get("WSET", "all") == "in"
    if not NOW:
        w_in = dram("w_in", [DEPTH, D, INW], F32, "ExternalInput")
    if not NOW and not WIN_ONLY:
        w_pw = dram("w_pw", [DEPTH, CC, CC], F32, "ExternalInput")
        w_out = dram("w_out", [DEPTH, D, D], F32, "ExternalInput")
        w_up = dram("w_up", [DEPTH, D, 2 * DFF], F32, "ExternalInput")
        w_dn = dram("w_dn", [DEPTH, DFF, D], F32, "ExternalInput")
    cvec = dram("cvec", [DEPTH, 128, NCV], F32, "ExternalInput")
    consts = dram("consts", [128, NK], F32, "ExternalInput")
    out = dram("out", [NSEQ, S, D], F32, "ExternalOutput")
    win_s = dram("win_s", [DEPTH, 40, 128, 2048], BF16, "Internal")
    wba_s = dram("wba_s", [DEPTH, 128, 256], BF16, "Internal")
    wpw_s = dram("wpw_s", [DEPTH, 8, 128, 1024], BF16, "Internal")
    wout_s = dram("wout_s", [DEPTH, 16, 128, 2048], BF16, "Internal")
    wup_s = dram("wup_s", [DEPTH, 88, 128, 2048], BF16, "Internal")
    wdn_s = dram("wdn_s", [DEPTH, 16, 128, DFF], BF16, "Internal")
    wdg_s = dram("wdg_s", [DEPTH, 8, 128, 31 * 128], BF16, "Internal")

    def sb(name, shape, dt):
        t = es.enter_context(nc.sbuf_tensor(name, list(shape), dt))
        return t.ap() if hasattr(t, "ap") else t[:]

    xT = sb("xT", [128, 16, T], F32)
    xTb = [Buf() for _ in range(16)]
    hT = sb("hT", [128, 16, T], BF16)
    hTb = [Buf() for _ in range(16)]
    mixcat = sb("mixcat", [128, 16, T], BF16)
    mixb = [Buf() for _ in range(16)]
    actA = sb("actA", [128, 8, T], BF16)
    actAb = [Buf() for _ in range(8)]
    cst = sb("cst", [128, NK], F32)
    cstb = Buf()
    cvt = sb("cvt", [128, DEPTH, NCV], F32)
    cvtb = Buf()
    onesb = sb("onesb", [128, 128], BF16)
    negA = sb("negA", [128, DEPTH, 8], F32)
    histu = sb("histu", [128, DEPTH, 8, 30], BF16)
    histq = sb("histq", [128, DEPTH, 16, 3], F32)
    histf = sb("histf", [128, DEPTH, NFC, 2], F32)
    histub = [[Buf() for _ in range(8)] for _ in range(DEPTH)]
    histqb = [[Buf() for _ in range(16)] for _ in range(DEPTH)]
    histfb = [[Buf() for _ in range(NFC)] for _ in range(DEPTH)]
    Sst = sb("Sst", [128, DEPTH, 8, 128], F32)
    Sb = [[Buf() for _ in range(8)] for _ in range(DEPTH)]
    wring = Ring([Slot(sb(f"wsl{i}", [128, 4096], BF16)) for i in range(4)])
    wbar = Ring([Slot(sb(f"wba{i}", [128, 256], BF16)) for i in range(2)])
    tmpF = Ring([Slot(sb(f"tF{i}", [128, T], F32)) for i in range(10)])
    tmpB = Ring([Slot(sb(f"tB{i}", [128, T], BF16)) for i in range(8)])
    cring = Ring([Slot(sb(f"cb{i}", [128, T + 32], F32)) for i in range(5)])
    ubring = Ring([Slot(sb(f"ub{i}", [128, T + 32], BF16)) for i in range(4)])
    ringAD = Ring([Slot(sb(f"rAD{i}", [128, T], F32)) for i in range(4)])
    ringACC = Ring([Slot(sb(f"rAC{i}", [128, T], F32)) for i in range(6)])
    ringS = Ring([Slot(sb(f"rS{i}", [128, T], F32)) for i in range(6)])
    small = sb("small", [128, 256], F32)
    smb = {}

    def sm(name):
        if name not in smb:
            smb[name] = Buf()
        return smb[name]

    A_QK = 0
    A_VT = A_QK + 8 * T
    A_ZS = A_VT + NB * 8 * 128
    A_G = A_ZS + 8 * T // 2
    GW = 4 * 1024 + 8 * 512 + 3 * 1024 + 1024
    A_END = A_G + GW + 1024
    ARENA = max(A_END, NFC * T // 2, 16 * T, 6 * 2048 + 6 * 1024)
    arena = sb("arena", [128, ARENA], F32)
    qkT = arena[:, A_QK:A_QK + 8 * T].rearrange("p (c t) -> p c t", c=8)
    qkb = [Buf() for _ in range(8)]
    v_tok = arena[:, A_VT:A_VT + NB * 8 * 128].rearrange("p (b h d) -> p b h d", b=NB, h=8)
    vtb = [Buf() for _ in range(8)]
    zs = arena[:, A_ZS:A_ZS + 8 * T // 2].bitcast(BF16).rearrange("p (c t) -> p c t", c=8)
    zsb = [Buf() for _ in range(8)]
    o = A_G
    tmpG = Ring([Slot(arena[:, o + i * 1024:o + (i + 1) * 1024].rearrange("p (h d) -> p h d", h=8)) for i in range(4)])
    o += 4096
    Pbuf = [[Slot(arena[:, o + (hf * 2 + pp) * 512:o + (hf * 2 + pp + 1) * 512].rearrange("p (h d) -> p h d", h=4))
             for pp in range(2)] for hf in range(2)]
    o += 2048
    PTbuf = [[Slot(arena[:, o + (hf * 2 + pp) * 512:o + (hf * 2 + pp + 1) * 512].rearrange("p (h d) -> p h d", h=4))
              for pp in range(2)] for hf in range(2)]
    o += 2048
    TTt = Slot(arena[:, o:o + 1024].rearrange("p (h d) -> p h d", h=8)); o += 1024
    TTb = [Buf() for _ in range(8)]
    QKDT = Slot(arena[:, o:o + 1024].rearrange("p (h d) -> p h d", h=8)); o += 1024
    qgt = Slot(arena[:, o:o + 1024].rearrange("p (h d) -> p h d", h=8)); o += 1024
    kdec = Slot(arena[:, o:o + 1024].rearrange("p (h d) -> p h d", h=8)); o += 1024
    gcbd = Slot(arena[:, o:o + 1024]); o += 1024
    actf = arena[:, 0:NFC * T // 2].bitcast(BF16).rearrange("p (c t) -> p c t", c=NFC)
    actfb = [Buf() for _ in range(NFC)]
    yT = arena[:, 0:16 * T].rearrange("p (c t) -> p c t", c=16)
    yTb = [Buf() for _ in range(16)]
    NST = 6
    sin_ = Ring([Slot(arena[:, i * 2048:(i + 1) * 2048]) for i in range(NST)])
    sout = Ring([Slot(arena[:, NST * 2048 + i * 1024:NST * 2048 + (i + 1) * 1024].bitcast(BF16)) for i in range(NST)])

    def pst(i):
        t = es.enter_context(nc.psum_tensor(f"ps{i}", [128, 512], F32))
        return t.ap() if hasattr(t, "ap") else t[:]
    ps = [Slot(pst(i)) for i in range(8)]
    ring_mix = Ring(ps[0:3] + [ps[6]])
    PS = ps[3]
    PSring = Ring([ps[3], ps[7]])
    gring = Ring(ps[4:8])
    gring2 = Ring(ps[4:6])
    ring_ffn = Ring(ps[0:3] + ps[4:8])

    ident = cst[:, K_ID:K_ID + 128]
    tri = cst[:, K_TRI:K_TRI + 128]
    onesf = cst[:, K_ONE:K_ONE + 128]
    maskneg = cst[:, K_MNEG:K_MNEG + 128]
    strict = cst[:, K_STR:K_STR + 128]
    ind = cst[0:8, K_IND:K_IND + 1024]

    def E(eng_name, e):
        return e

    def mm(out_, lhsT, rhs, start, stop, r, w):
        P.op("pe", lambda e: e.matmul(out_, lhsT, rhs, start=start, stop=stop), r, w)

    def tr(out_, in_, r, w):
        P.op("pe", lambda e: e.transpose(out_, in_, ident), list(r) + [cstb], w)

    def act(out_, in_, func, r, w, bias=None, scale=None, accum=None):
        kw = {}
        if bias is not None:
            kw["bias"] = bias
        if scale is not None:
            kw["scale"] = scale
        if accum is not None:
            kw["accum_out"] = accum
        P.op("act", lambda e: e.activation(out_, in_, func, **kw), r, w)

    def tt(eng, out_, a, b, op, r, w):
        P.op(eng, lambda e: e.tensor_tensor(out_, a, b, op), r, w)

    def ts(eng, out_, a, s1, s2, op0, op1, r, w):
        if op1 is None:
            P.op(eng, lambda e: e.tensor_scalar(out_, a, s1, None, op0), r, w)
        else:
            P.op(eng, lambda e: e.tensor_scalar(out_, a, s1, s2, op0, op1), r, w)

    def stt(eng, out_, a, s, b, op0, op1, r, w):
        eng = "dve"
        P.op(eng, lambda e: e.scalar_tensor_tensor(out_, a, s, b, op0, op1), r, w)

    def cp(eng, out_, in_, r, w):
        if eng == "act":
            P.op("act", lambda e: e.copy(out_, in_), r, w)
        else:
            P.op(eng, lambda e: e.tensor_copy(out_, in_), r, w)

    def mset(eng, ap, val, w):
        P.op(eng, lambda e: e.memset(ap, val), (), w)

    def recip(out_, in_, r, w):
        P.op("dve", lambda e: e.reciprocal(out_, in_), r, w)

    def dma(q, out_, in_, r, w, ctr):
        return P.op(q, lambda e: e.dma_start(out=out_, in_=in_), r, w, dma=True, ctr=ctr)

    dma("sp", cst[:, :], consts[:, :], [], [cstb], cstb)
    for l in range(DEPTH):
        dma("sp", cvt[:, l, :], cvec[l, :, :], [], [cvtb], cvtb)
    cp("dve", onesb[:, :], onesf, [cstb], [sm("onesb")])
    for l in range(DEPTH):
        act(negA[:, l, :], cvt[:, l, O_ALOG:O_ALOG + 8], AF.Exp, [cvtb], [sm("negA")])
        ts("dve", negA[:, l, :], negA[:, l, :], -1.0, None, OP.mult, None, [sm("negA")], [sm("negA")])

    def cv0(l, col, n=1):
        return cvt[:, l, col:col + n]

    pieces = []
    order = []
    oth = [("q", c, 2048 + c * 128) for c in range(4)] + [("k", c, 2560 + c * 128) for c in range(4)] + \
          [("v", c, 3072 + c * 128) for c in range(8)] + [("z", c, 4096 + c * 128) for c in range(8)]
    for c in range(8):
        order += [("av", c, c * 128), ("ag", c, 1024 + c * 128)]
    for c in range(8):
        pass
    order2 = []
    for c in range(8):
        order2 += [("av", c, c * 128), ("ag", c, 1024 + c * 128), oth[2 * c], oth[2 * c + 1]]
    order2 += oth[16:24]
    order = order2
    for l in range(0 if NOW else DEPTH):
        for ci, (_, _, c0) in enumerate(order):
            pieces.append((w_in[l, :, c0:c0 + 128].rearrange("(k p) n -> p k n", p=128), (16, 128), win_s[l, ci]))
        pieces.append((w_in[l, :, 5120:5136].rearrange("(k p) n -> p k n", p=128), (16, 16), wba_s[l]))
        if WIN_ONLY:
            continue
        for m in range(8):
            pieces.append((w_pw[l, :, m * 128:(m + 1) * 128].rearrange("(k p) n -> p k n", p=128), (8, 128), wpw_s[l, m]))
        for m in range(16):
            pieces.append((w_out[l, :, m * 128:(m + 1) * 128].rearrange("(k p) n -> p k n", p=128), (16, 128), wout_s[l, m]))
        for c in range(NFC):
            for hf in range(2):
                c0 = hf * DFF + c * 128
                pieces.append((w_up[l, :, c0:c0 + 128].rearrange("(k p) n -> p k n", p=128), (16, 128), wup_s[l, 2 * c + hf]))
        for m in range(16):
            for q4 in range(4):
                pieces.append((w_dn[l, q4 * 1408:(q4 + 1) * 1408, m * 128:(m + 1) * 128].rearrange("(k p) n -> p k n", p=128),
                               (11, 128), wdn_s[l, m][:, q4 * 1408:(q4 + 1) * 1408]))
    stores = []
    ceng = ["act", "dve"]
    if cfg.get("STOP", 99) <= -1:
        pieces = pieces[:cfg.get("NPIECE", 0)]
    SUB = cfg.get("SUB", 99)
    for i, (src, (kk, nn), dst) in enumerate(pieces):
        n = kk * nn
        si = sin_.next()
        so = sout.next()
        dma("sp", si.ap[:, 0:n].rearrange("p (k n) -> p k n", k=kk), src, [], [si.buf], si.buf)
        cp(ceng[i % 2], so.ap[:, 0:n], si.ap[:, 0:n], [si.buf], [so.buf])
        stores.append(dma("pool", dst, so.ap[:, 0:n], [so.buf], [], so.buf))
    dstores = []
    hbufs = {}
    if not NOW:
        for l in range(DEPTH):
            for c in range(8):
                sl = wring.next()
                hb = hbufs.setdefault(id(sl), [Buf(), Buf()])
                for k in range(31):
                    if k % 2 == 0:
                        ts("dve", sl.ap[:, k * 128:(k + 1) * 128], ident, cv0(l, O_DWW + c * 31 + k), None,
                           OP.mult, None, [cstb, cvtb], [hb[0]])
                    else:
                        act(sl.ap[:, k * 128:(k + 1) * 128], ident, AF.Identity, [cstb, cvtb], [hb[1]],
                            scale=cv0(l, O_DWW + c * 31 + k))
                dstores.append(dma("sp", wdg_s[l, c], sl.ap[:, 0:31 * 128], [sl.buf, hb[0], hb[1]], [], sl.buf))
    P.barrier(extra=stores[-NST:] + dstores)

    def cv(l, col, n=1):
        return cvt[:, l, col:col + n]

    def rmsnorm(l, gcol, final=False):
        for kc in range(16):
            sq = tmpB.next()
            act(sq.ap, xT[:, kc, :], AF.Square, [xTb[kc]], [sq.buf])
            mm(PS.ap[:, 0:T], onesb[:, :], sq.ap, kc == 0, kc == 15, [sq.buf, sm("onesb")], [PS.buf])
        v = tmpF.next()
        ts("dve", v.ap, PS.ap[:, 0:T], 1.0 / D, EPS, OP.mult, OP.add, [PS.buf], [v.buf])
        act(v.ap, v.ap, AF.Sqrt, [v.buf], [v.buf])
        recip(v.ap, v.ap, [v.buf], [v.buf])
        for kc in range(16):
            eng = "dve" if kc % 2 == 0 else "pool"
            if final:
                stt(eng, yT[:, kc, :], xT[:, kc, :], cv(l, gcol + kc), v.ap, OP.mult, OP.mult,
                    [xTb[kc], v.buf, cvtb], [yTb[kc]])
            else:
                stt(eng, hT[:, kc, :], xT[:, kc, :], cv(l, gcol + kc), v.ap, OP.mult, OP.mult,
                    [xTb[kc], v.buf, cvtb], [hTb[kc]])

    def conv_small(l, first, bank_ap, bank_buf, hist_ap, hist_buf, ntap, wcol, bcol, eng="dve"):
        H = ntap - 1
        cb = cring.next()
        if first:
            mset("pool", cb.ap[:, 0:H], 0.0, [cb.buf])
        else:
            cp("pool", cb.ap[:, 0:H], hist_ap, [hist_buf], [cb.buf])
        cp("act", cb.ap[:, H:H + T], bank_ap, [bank_buf], [cb.buf])
        cp("pool", hist_ap, cb.ap[:, T:T + H], [cb.buf], [hist_buf])
        acc = tmpF.next()
        if bcol is None:
            ts(eng, acc.ap, cb.ap[:, 0:T], cv(l, wcol), None, OP.mult, None, [cb.buf, cvtb], [acc.buf])
        else:
            ts(eng, acc.ap, cb.ap[:, 0:T], cv(l, wcol), cv(l, bcol), OP.mult, OP.add, [cb.buf, cvtb], [acc.buf])
        for k in range(1, ntap):
            stt(eng, acc.ap, cb.ap[:, k:k + T], cv(l, wcol + k), acc.ap, OP.mult, OP.add, [cb.buf, acc.buf, cvtb], [acc.buf])
        return acc

    class Pipe:
        def __init__(self):
            self.q = []
            self.step = 0
            self.n = 0

        def defer(self, delay, fn):
            self.q.append((self.step + delay, self.n, fn))
            self.n += 1

        def tick(self):
            self.step += 1
            due = sorted([e for e in self.q if e[0] <= self.step])
            self.q = [e for e in self.q if e[0] > self.step]
            for _, _, f in due:
                f()

        def flush(self):
            while self.q:
                self.tick()

    pipe = Pipe()
    pshalf = [None]

    def st_conf(l, first, c, bank):
        sg = tmpF.next()
        act(sg.ap, bank.ap[:, T:2 * T], AF.Sigmoid, [bank.buf], [sg.buf])
        ub = ubring.take()
        hb = histub[l][c]
        if first:
            mset("pool", ub.ap[:, 0:30], 0.0, [ub.buf])
        else:
            cp("pool", ub.ap[:, 0:30], histu[:, l, c, :], [hb], [ub.buf])
        tt("dve", ub.ap[:, 30:30 + T], bank.ap[:, 0:T], sg.ap, OP.mult, [bank.buf, sg.buf], [ub.buf])
        cp("pool", histu[:, l, c, :], ub.ap[:, T:T + 30], [ub.buf], [hb])

        def d2():
            dsl = load_w(wdg_s[l, c])
            cbank = ring_mix.next()
            for k in range(31):
                mm(cbank.ap[:, 0:T], dsl.ap[:, k * 128:(k + 1) * 128], ub.ap[:, k:k + T], k == 0, k == 30, [dsl.buf, ub.buf], [cbank.buf])
            ub.free()
            aD = ringAD.take()
            act(aD.ap, cbank.ap[:, 0:T], AF.Identity, [cbank.buf, cvtb], [aD.buf], bias=cv(l, O_DWB + c))
            yb = tmpB.take()
            sqb = tmpB.take()
            cp("pool", yb.ap, aD.ap, [aD.buf], [yb.buf])
            act(sqb.ap, aD.ap, AF.Square, [aD.buf], [sqb.buf])

            def d3():
                psb = PSring.take()
                mm(psb.ap[:, 0:T], onesb[:, :], yb.ap, True, True, [yb.buf, sm("onesb")], [psb.buf])
                mm(psb.ap[:, T:2 * T], onesb[:, :], sqb.ap, True, True, [sqb.buf, sm("onesb")], [psb.buf])
                yb.free()
                sqb.free()

                def d4():
                    m2 = ringS.take()
                    act(m2.ap, psb.ap[:, 0:T], AF.Square, [psb.buf], [m2.buf], scale=1.0 / 128)
                    stt("dve", m2.ap, psb.ap[:, T:2 * T], 1.0 / 128, m2.ap, OP.mult, OP.subtract, [psb.buf, m2.buf], [m2.buf])
                    ts("dve", m2.ap, m2.ap, EPS, None, OP.add, None, [m2.buf], [m2.buf])
                    stt("dve", aD.ap, psb.ap[:, 0:T], -1.0 / 128, aD.ap, OP.mult, OP.add, [psb.buf, aD.buf], [aD.buf])
                    psb.free()

                    def d5():
                        act(m2.ap, m2.ap, AF.Sqrt, [m2.buf], [m2.buf])
                        recip(m2.ap, m2.ap, [m2.buf], [m2.buf])
                        tt("pool", aD.ap, aD.ap, m2.ap, OP.mult, [aD.buf, m2.buf], [aD.buf])
                        m2.free()

                        def d6():
                            act(actA[:, c, :], aD.ap, AF.Silu, [aD.buf, cvtb], [actAb[c]], scale=cv(l, O_LNG + c), bias=cv(l, O_LNB + c))
                            aD.free()
                        pipe.defer(1, d6)
                    pipe.defer(1, d5)
                pipe.defer(1, d4)
            pipe.defer(1, d3)
        pipe.defer(2, d2)

    def st_qkv(l, first, kind, c, bank):
        ch = {"q": c, "k": 4 + c, "v": 8 + c}[kind]
        hist_ap, hist_buf = histq[:, l, ch, :], histqb[l][ch]
        cb = cring.take()
        if first:
            mset("pool", cb.ap[:, 0:3], 0.0, [cb.buf])
        else:
            cp("pool", cb.ap[:, 0:3], hist_ap, [hist_buf], [cb.buf])
        cp("act", cb.ap[:, 3:3 + T], bank.ap[:, 0:T], [bank.buf], [cb.buf])
        cp("pool", hist_ap, cb.ap[:, T:T + 3], [cb.buf], [hist_buf])

        def d1():
            wcol = O_GCW + ch * 4
            acc = ringACC.take()
            ts("dve", acc.ap, cb.ap[:, 0:T], cv(l, wcol), None, OP.mult, None, [cb.buf, cvtb], [acc.buf])
            for k in range(1, 4):
                stt("dve", acc.ap, cb.ap[:, k:k + T], cv(l, wcol + k), acc.ap, OP.mult, OP.add, [cb.buf, acc.buf, cvtb], [acc.buf])
            act(acc.ap, acc.ap, AF.Silu, [acc.buf], [acc.buf])
            cb.free()
            if kind == "v":
                def d2v():
                    gb = gring2.take()
                    for b in range(NB):
                        tr(gb.ap[:, b * 128:(b + 1) * 128], acc.ap[:, b * 128:(b + 1) * 128], [acc.buf], [gb.buf])
                    acc.free()

                    def d3v():
                        cp("act", v_tok[:, :, c, :], gb.ap[:, 0:NB * 128].rearrange("p (b d) -> p b d", b=NB), [gb.buf], [vtb[c]])
                        gb.free()
                    pipe.defer(1, d3v)
                pipe.defer(1, d2v)
                return
            sqb = tmpB.take()
            act(sqb.ap, acc.ap, AF.Square, [acc.buf], [sqb.buf])

            def d2():
                if pshalf[0] is None:
                    psb = PSring.take(2)
                    pshalf[0] = psb
                    po = 0
                else:
                    psb = pshalf[0]
                    pshalf[0] = None
                    po = T
                mm(psb.ap[:, po:po + T], onesb[:, :], sqb.ap, True, True, [sqb.buf, sm("onesb")], [psb.buf])
                sqb.free()

                def d3():
                    rs = ringS.take()
                    ts("dve", rs.ap, psb.ap[:, po:po + T], EPS, None, OP.add, None, [psb.buf], [rs.buf])
                    psb.free()
                    act(rs.ap, rs.ap, AF.Sqrt, [rs.buf], [rs.buf])

                    def d4():
                        recip(rs.ap, rs.ap, [rs.buf], [rs.buf])
                        if kind == "q":
                            stt("dve", qkT[:, ch, :], acc.ap, 128.0 ** -0.5, rs.ap, OP.mult, OP.mult, [acc.buf, rs.buf], [qkb[ch]])
                        else:
                            tt("dve", qkT[:, ch, :], acc.ap, rs.ap, OP.mult, [acc.buf, rs.buf], [qkb[ch]])
                        rs.free()
                        acc.free()
                    pipe.defer(1, d4)
                pipe.defer(1, d3)
            pipe.defer(1, d2)
        pipe.defer(1, d1)

    def post_z(l, c, bank):
        act(zs[:, c, :], bank.ap[:, 0:T], AF.Silu, [bank.buf], [zsb[c]])

    SM_B, SM_NB, SM_G = 0, 8 * NB, 16 * NB
    SM_GG = 24 * NB
    SM_EGC = SM_GG + 16
    SM_NEGC = SM_EGC + 8
    SM_EGL = SM_NEGC + 8
    SM_EDEC = SM_EGL + 8
    SM_SSQ = SM_EDEC + 8
    SM_X = SM_SSQ + 8
    gcTs = sb("gcTs", [8, 128], F32)

    def ba_proj(l):
        wb = wbar.next()
        dma("sp", wb.ap, wba_s[l], [], [wb.buf], wb.buf)
        gb = gring.next()
        wv = wb.ap.rearrange("p (k n) -> p k n", k=16)
        for b in range(NB):
            for kc in range(16):
                mm(gb.ap[:, b * 16:(b + 1) * 16], hT[:, kc, b * 128:(b + 1) * 128], wv[:, kc, :], kc == 0, kc == 15,
                   [hTb[kc], wb.buf], [gb.buf])
        for b in range(NB):
            bsl = small[:, SM_B + b * 8:SM_B + (b + 1) * 8]
            act(bsl, gb.ap[:, b * 16:b * 16 + 8], AF.Sigmoid, [gb.buf], [sm("btok")])
            ts("pool", small[:, SM_NB + b * 8:SM_NB + (b + 1) * 8], bsl, -1.0, None, OP.mult, None, [sm("btok")], [sm("nbtok")])
            xx = small[:, SM_X + b * 8:SM_X + (b + 1) * 8]
            tt("dve", xx, gb.ap[:, b * 16 + 8:b * 16 + 16], cv(l, O_DTB, 8), OP.add, [gb.buf, cvtb], [sm("xx")])
            act(xx, xx, AF.Exp, [sm("xx")], [sm("xx")])
            act(xx, xx, AF.Ln, [sm("xx")], [sm("xx")], bias=1.0)
            tt("dve", small[:, SM_G + b * 8:SM_G + (b + 1) * 8], xx, negA[:, l, :], OP.mult, [sm("xx"), sm("negA")], [sm("gtok")])

    def gdn_prep(l, b):
        bs = slice(b * 128, (b + 1) * 128)
        gsl = small[:, SM_G + b * 8:SM_G + (b + 1) * 8]
        X1 = gring.next()
        mm(X1.ap[:, 0:8], tri, gsl, True, True, [cstb, sm("gtok")], [X1.buf])
        mm(X1.ap[:, 8:16], onesf, gsl, True, True, [cstb, sm("gtok")], [X1.buf])
        mm(X1.ap[0:8, 16:144], gsl, tri, True, True, [cstb, sm("gtok")], [X1.buf])
        gg = small[:, SM_GG:SM_GG + 16]
        cp("act", gg, X1.ap[:, 0:16], [X1.buf], [sm("gg")])
        cp("dve", gcTs[:, :], X1.ap[0:8, 16:144], [X1.buf], [sm("gcTs")])
        if SUB <= 0.2:
            return
        act(small[:, SM_EGC:SM_EGC + 8], gg[:, 0:8], AF.Exp, [sm("gg")], [sm("egc")])
        ts("dve", small[:, SM_NEGC:SM_NEGC + 8], small[:, SM_EGC:SM_EGC + 8], -1.0, None, OP.mult, None, [sm("egc")], [sm("negc")])
        act(small[:, SM_EGL:SM_EGL + 8], gg[:, 8:16], AF.Exp, [sm("gg")], [sm("egl")])
        tt("dve", small[:, SM_EDEC:SM_EDEC + 8], gg[:, 8:16], gg[:, 0:8], OP.subtract, [sm("gg")], [sm("edec")])
        act(small[:, SM_EDEC:SM_EDEC + 8], small[:, SM_EDEC:SM_EDEC + 8], AF.Exp, [sm("edec")], [sm("edec")])
        if SUB <= 0.4:
            return
        gv = gcbd.ap[0:8, :].rearrange("p (h d) -> p h d", h=8)
        indv = ind.rearrange("p (h d) -> p h d", h=8)
        for h in range(8):
            tt("dve", gv[:, h, :], indv[:, h, :], gcTs[:, :], OP.mult, [cstb, sm("gcTs")], [gcbd.buf])
        if SUB <= 0.6:
            return
        G2 = [gring.next(), gring.next()]
        for hf in range(2):
            mm(G2[hf].ap[:, :], onesf[0:8, :], gcbd.ap[0:8, hf * 512:(hf + 1) * 512], True, True, [cstb, gcbd.buf], [G2[hf].buf])
        if SUB <= 0.8:
            return
        EG = tmpG.next()
        E1 = tmpG.next()
        for hf in range(2):
            act(EG.ap[:, hf * 4:(hf + 1) * 4, :], G2[hf].ap.rearrange("p (h d) -> p h d", h=4), AF.Exp, [G2[hf].buf], [EG.buf])
        if SUB <= 0.85:
            return
        ngc = small[:, SM_X + 16:SM_X + 24]
        ts("dve", ngc, gg[:, 0:8], -1.0, None, OP.mult, None, [sm("gg")], [sm("ngc")])
        for hf in range(2):
            act(E1.ap[:, hf * 4:(hf + 1) * 4, :], G2[hf].ap.rearrange("p (h d) -> p h d", h=4), AF.Identity, [G2[hf].buf], [E1.buf])
        for h in range(8):
            stt("dve", E1.ap[:, h, :], E1.ap[:, h, :], ngc[:, h:h + 1], maskneg,
                OP.add, OP.add, [E1.buf, sm("ngc"), cstb], [E1.buf])
        if SUB <= 0.9:
            return
        act(E1.ap, E1.ap, AF.Exp, [E1.buf], [E1.buf])
        if SUB <= 1:
            return
        for h in range(8):
            tt("dve", qgt.ap[:, h, :], qkT[:, h // 2, bs], EG.ap[:, h, :], OP.mult, [qkb[h // 2], EG.buf], [qgt.buf])
        Y = [gring.next(), gring.next()]
        for hq in range(4):
            kb = qkT[:, 4 + hq, bs]
            qb = qkT[:, hq, bs]
            yo = (hq % 2) * 256
            mm(Y[hq // 2].ap[:, yo:yo + 128], kb, kb, True, True, [qkb[4 + hq]], [Y[hq // 2].buf])
            mm(Y[hq // 2].ap[:, yo + 128:yo + 256], kb, qb, True, True, [qkb[4 + hq], qkb[hq]], [Y[hq // 2].buf])
        for h in range(8):
            hq = h // 2
            yo = (hq % 2) * 256
            tt("dve", QKDT.ap[:, h, :], Y[hq // 2].ap[:, yo + 128:yo + 256], E1.ap[:, h, :], OP.mult, [Y[hq // 2].buf, E1.buf], [QKDT.buf])
        for h in range(8):
            tt("dve", E1.ap[:, h, :], E1.ap[:, h, :], strict, OP.mult, [E1.buf, cstb], [E1.buf])
        for h in range(8):
            hq = h // 2
            yo = (hq % 2) * 256
            pt0 = PTbuf[h // 4][0]
            stt("dve", pt0.ap[:, h % 4, :], Y[hq // 2].ap[:, yo:yo + 128], small[:, SM_NB + b * 8 + h:SM_NB + b * 8 + h + 1],
                E1.ap[:, h, :], OP.mult, OP.mult, [Y[hq // 2].buf, sm("nbtok"), E1.buf], [pt0.buf])
        for h in range(8):
            tt("dve", TTt.ap[:, h, :], PTbuf[h // 4][0].ap[:, h % 4, :], ident, OP.add, [PTbuf[h // 4][0].buf, cstb], [TTb[h]])
        if SUB <= 2:
            return
        Kt = gring.next()
        for hq in range(4):
            tr(Kt.ap[:, hq * 128:(hq + 1) * 128], qkT[:, 4 + hq, bs], [qkb[4 + hq]], [Kt.buf])
        for h in range(8):
            act(kdec.ap[:, h, :], Kt.ap[:, (h // 2) * 128:(h // 2 + 1) * 128], AF.Identity, [Kt.buf, sm("edec")], [kdec.buf],
                scale=small[:, SM_EDEC + h:SM_EDEC + h + 1])
        if SUB <= 3:
            return
        for hf in range(2):
            Pb = gring.next()
            for j in range(4):
                tr(Pb.ap[:, j * 128:(j + 1) * 128], PTbuf[hf][0].ap[:, j, :], [PTbuf[hf][0].buf], [Pb.buf])
            cp("act", Pbuf[hf][0].ap, Pb.ap.rearrange("p (h d) -> p h d", h=4), [Pb.buf], [Pbuf[hf][0].buf])
        if SUB <= 4:
            return
        cur = [0, 0]
        for k in range(1, 7):
            for hf in range(2):
                c_ = cur[hf]
                n_ = 1 - c_
                Pc, PTc, Pn, PTn = Pbuf[hf][c_], PTbuf[hf][c_], Pbuf[hf][n_], PTbuf[hf][n_]
                A = gring.next()
                for j in range(4):
                    mm(A.ap[:, j * 128:(j + 1) * 128], PTc.ap[:, j, :], Pc.ap[:, j, :], True, True, [PTc.buf, Pc.buf], [A.buf])
                cp("act", Pn.ap, A.ap.rearrange("p (h d) -> p h d", h=4), [A.buf], [Pn.buf])
                if k < 6:
                    B = gring.next()
                    for j in range(4):
                        mm(B.ap[:, j * 128:(j + 1) * 128], Pc.ap[:, j, :], PTc.ap[:, j, :], True, True, [PTc.buf, Pc.buf], [B.buf])
                    cp("dve", PTn.ap, B.ap.rearrange("p (h d) -> p h d", h=4), [B.buf], [PTn.buf])
                C = gring.next()
                for j in range(4):
                    h = hf * 4 + j
                    mm(C.ap[:, j * 128:(j + 1) * 128], Pn.ap[:, j, :], TTt.ap[:, h, :], True, True, [Pn.buf, TTb[h]], [C.buf])
                for j in range(4):
                    h = hf * 4 + j
                    tt("dve", TTt.ap[:, h, :], TTt.ap[:, h, :], C.ap[:, j * 128:(j + 1) * 128], OP.add, [TTb[h], C.buf], [TTb[h]])
                cur[hf] = n_

    def gdn_chain(l, b):
        bs = slice(b * 128, (b + 1) * 128)
        if SUB <= 5:
            return

        def pair():
            return [gring.next(), gring.next()]

        def pv(banks, h):
            return banks[h // 4].ap[:, (h % 4) * 128:(h % 4 + 1) * 128], banks[h // 4].buf

        KS = pair()
        for h in range(8):
            o_, ob = pv(KS, h)
            mm(o_, qkT[:, 4 + h // 2, bs], Sst[:, l, h, :], True, True, [qkb[4 + h // 2], Sb[l][h]], [ob])
        R = tmpG.next()
        for h in range(8):
            o_, ob = pv(KS, h)
            stt("dve", R.ap[:, h, :], o_, small[:, SM_NEGC + h:SM_NEGC + h + 1], v_tok[:, b, h, :], OP.mult, OP.add,
                [ob, sm("negc"), vtb[h]], [R.buf])
        VN = pair()
        for h in range(8):
            o_, ob = pv(VN, h)
            mm(o_, TTt.ap[:, h, :], R.ap[:, h, :], True, True, [TTb[h], R.buf], [ob])
        vn = tmpG.next()
        for h in range(8):
            o_, ob = pv(VN, h)
            act(vn.ap[:, h, :], o_, AF.Identity, [ob, sm("btok")], [vn.buf], scale=small[:, SM_B + b * 8 + h:SM_B + b * 8 + h + 1])
        if SUB <= 6:
            return
        O = pair()
        for h in range(8):
            o_, ob = pv(O, h)
            mm(o_, qgt.ap[:, h, :], Sst[:, l, h, :], True, False, [qgt.buf, Sb[l][h]], [ob])
            mm(o_, QKDT.ap[:, h, :], vn.ap[:, h, :], False, True, [QKDT.buf, vn.buf], [ob])
        SU = pair()
        for h in range(8):
            o_, ob = pv(SU, h)
            mm(o_, kdec.ap[:, h, :], vn.ap[:, h, :], True, True, [kdec.buf, vn.buf], [ob])
        for h in range(8):
            o_, ob = pv(SU, h)
            stt("dve", Sst[:, l, h, :], Sst[:, l, h, :], small[:, SM_EGL + h:SM_EGL + h + 1], o_, OP.mult, OP.add,
                [Sb[l][h], sm("egl"), ob], [Sb[l][h]])
        if SUB <= 7:
            return
        on = tmpG.next()
        junk = tmpF.next()
        ssq = small[:, SM_SSQ:SM_SSQ + 8]
        for h in range(8):
            o_, ob = pv(O, h)
            act(junk.ap[:, 0:128], o_, AF.Square, [ob], [junk.buf, sm("ssq")], accum=ssq[:, h:h + 1])
        ts("dve", ssq, ssq, 1.0 / 128, EPS, OP.mult, OP.add, [sm("ssq")], [sm("ssq")])
        act(ssq, ssq, AF.Sqrt, [sm("ssq")], [sm("ssq")])
        recip(ssq, ssq, [sm("ssq")], [sm("ssq")])
        for h in range(8):
            o_, ob = pv(O, h)
            act(on.ap[:, h, :], o_, AF.Identity, [ob, sm("ssq")], [on.buf], scale=ssq[:, h:h + 1])
        if SUB <= 8:
            return
        OT = pair()
        for h in range(8):
            o_, ob = pv(OT, h)
            tr(o_, on.ap[:, h, :], [on.buf], [ob])
        for h in range(8):
            o_, ob = pv(OT, h)
            stt("dve", mixcat[:, 8 + h, bs], o_, cv(l, O_GNG), zs[:, h, bs], OP.mult, OP.mult, [ob, cvtb, zsb[h]], [mixb[8 + h]])

    def load_w(src, c=None):
        sl = wring.next()
        if c is None:
            dst = sl.ap[:, 0:src.shape[-1]]
        else:
            dst = sl.ap.rearrange("p (c f) -> p c f", c=c)
        dma("sp", dst, src, [], [sl.buf], sl.buf)
        return sl

    STOP = cfg.get("STOP", 99)

    def unit(seq, tile, l):
        first = tile == 0
        t0 = tile * T
        if STOP <= 0:
            return
        if l == 0:
            for b in range(NB):
                sl = wring.next()
                sf = sl.ap.bitcast(F32)
                dma("sp", sf, x[seq, t0 + b * 128:t0 + (b + 1) * 128, :], [], [sl.buf], sl.buf)
                for g in range(4):
                    gb = gring.next()
                    for j in range(4):
                        kc = g * 4 + j
                        tr(gb.ap[:, j * 128:(j + 1) * 128], sf[:, kc * 128:(kc + 1) * 128], [sl.buf], [gb.buf])
                    cp("act" if g % 2 == 0 else "dve", xT[:, g * 4:(g + 1) * 4, b * 128:(b + 1) * 128],
                       gb.ap.rearrange("p (j t) -> p j t", j=4), [gb.buf], [xTb[g * 4 + j] for j in range(4)])
        if first:
            for h in range(8):
                mset("pool", Sst[:, l, h, :], 0.0, [Sb[l][h]])
        if STOP <= 1:
            return
        rmsnorm(l, O_MIXG)
        if STOP <= 2:
            return
        ba_proj(l)
        if STOP <= 3:
            return
        for g in range(20):
            pipe.tick()
            sl = load_w(win_s[l, 2 * g:2 * g + 2].rearrange("c p f -> p c f"), 2)
            wv = sl.ap.rearrange("p (c k n) -> p c k n", c=2, k=16)
            kinds = order[2 * g:2 * g + 2]
            if kinds[0][0] == "av":
                bank = ring_mix.next()
                for j in range(2):
                    for kc in range(16):
                        mm(bank.ap[:, j * T:(j + 1) * T], wv[:, j, kc, :], hT[:, kc, :], kc == 0, kc == 15, [sl.buf, hTb[kc]], [bank.buf])
                st_conf(l, first, kinds[0][1], bank)
            else:
                for j in range(2):
                    bank = ring_mix.next()
                    for kc in range(16):
                        mm(bank.ap[:, 0:T], wv[:, j, kc, :], hT[:, kc, :], kc == 0, kc == 15, [sl.buf, hTb[kc]], [bank.buf])
                    kind, c, _ = kinds[j]
                    if kind in ("q", "k", "v"):
                        st_qkv(l, first, kind, c, bank)
                    else:
                        post_z(l, c, bank)
        pipe.flush()
        if STOP <= 4:
            return
        for g in range(2):
            sl = load_w(wpw_s[l, 4 * g:4 * g + 4].rearrange("c p f -> p c f"), 4)
            wv = sl.ap.rearrange("p (c k n) -> p c k n", c=4, k=8)
            for j in range(4):
                m = 4 * g + j
                bank = ring_mix.next()
                for kc in range(8):
                    mm(bank.ap[:, 0:T], wv[:, j, kc, :], actA[:, kc, :], kc == 0, kc == 7, [sl.buf, actAb[kc]], [bank.buf])
                act(mixcat[:, m, :], bank.ap[:, 0:T], AF.Identity, [bank.buf, cvtb], [mixb[m]], bias=cv(l, O_PWB + m))
        if STOP <= 5:
            return
        for b in range(NB):
            gdn_prep(l, b)
            gdn_chain(l, b)
        if STOP <= 6:
            return
        for g in range(8):
            sl = load_w(wout_s[l, 2 * g:2 * g + 2].rearrange("c p f -> p c f"), 2)
            wv = sl.ap.rearrange("p (c k n) -> p c k n", c=2, k=16)
            for j in range(2):
                m = 2 * g + j
                bank = ring_mix.next()
                for kc in range(16):
                    mm(bank.ap[:, 0:T], wv[:, j, kc, :], mixcat[:, kc, :], kc == 0, kc == 15, [sl.buf, mixb[kc]], [bank.buf])
                tt("dve", xT[:, m, :], xT[:, m, :], bank.ap[:, 0:T], OP.add, [xTb[m], bank.buf], [xTb[m]])
        P.barrier(sp=False)
        if STOP <= 7:
            return
        rmsnorm(l, O_FFNG)
        for c in range(NFC):
            sl = load_w(wup_s[l, 2 * c:2 * c + 2].rearrange("c p f -> p c f"), 2)
            wv = sl.ap.rearrange("p (c k n) -> p c k n", c=2, k=16)
            bank = ring_ffn.next()
            for j in range(2):
                for kc in range(16):
                    mm(bank.ap[:, j * T:(j + 1) * T], wv[:, j, kc, :], hT[:, kc, :], kc == 0, kc == 15, [sl.buf, hTb[kc]], [bank.buf])
            eng = "dve" if c % 2 == 0 else "pool"
            acc = conv_small(l, first, bank.ap[:, 0:T], bank.buf, histf[:, l, c, :], histfb[l][c], 3, O_FCW + c * 3, O_FCB + c, eng=eng)
            act(acc.ap, acc.ap, AF.Silu, [acc.buf], [acc.buf])
            tt("dve", actf[:, c, :], acc.ap, bank.ap[:, T:2 * T], OP.mult, [acc.buf, bank.buf], [actfb[c]])
        for m in range(16):
            bank = ring_ffn.next()
            for hf in range(2):
                sl = load_w(wdn_s[l, m][:, hf * 2816:(hf + 1) * 2816])
                wv = sl.ap[:, 0:2816].rearrange("p (k n) -> p k n", k=22)
                for kk in range(22):
                    kc = hf * 22 + kk
                    mm(bank.ap[:, 0:T], wv[:, kk, :], actf[:, kc, :], kc == 0, kc == NFC - 1, [sl.buf, actfb[kc]], [bank.buf])
            tt("dve", xT[:, m, :], xT[:, m, :], bank.ap[:, 0:T], OP.add, [xTb[m], bank.buf], [xTb[m]])
        P.barrier(sp=False)
        if l == DEPTH - 1:
            rmsnorm(l, O_FING, final=True)
            for b in range(NB):
                sl = wring.next()
                sf = sl.ap.bitcast(F32)
                for g in range(4):
                    gb = gring.next()
                    for j in range(4):
                        kc = g * 4 + j
                        tr(gb.ap[:, j * 128:(j + 1) * 128], yT[:, kc, b * 128:(b + 1) * 128], [yTb[kc]], [gb.buf])
                    cp("act" if g % 2 == 0 else "dve", sf[:, g * 512:(g + 1) * 512], gb.ap, [gb.buf], [sl.buf])
                dma("act", out[seq, t0 + b * 128:t0 + (b + 1) * 128, :], sf, [sl.buf], [], sl.buf)
            P.barrier(sp=False)

    for seq in range(NSEQ):
        for tile in range(NT):
            for l in range(DEPTH):
                unit(seq, tile, l)
    final_waits = []
    for sl in wring.items:
        for d in sl.buf.r.values():
            if d.dma:
                final_waits.append(d)
        if sl.buf.w is not None and sl.buf.w.dma:
            final_waits.append(sl.buf.w)
    P.barrier(extra=final_waits)

    nsig = {e: sum(1 for i in P.q[e] if i.signal and not i.dma) for e in ENGS}
    esems = {e: [es.enter_context(nc.semaphore(f"s_{e}_{k}")) for k in range(nsig[e] // EPOCH + 1)] for e in ENGS}
    nds = 0
    for i in P.dmas:
        b = i.ctr
        if b.ctr is None:
            b.ctr = [es.enter_context(nc.semaphore(f"dsem{nds}")), 0]
            nds += 1
        b.ctr[1] += 16
        i.sem, i.val = b.ctr[0], b.ctr[1]
    for e in ENGS:
        cnt = 0
        for i in P.q[e]:
            if i.dma:
                continue
            if i.signal:
                i.sem, i.val = esems[e][cnt // EPOCH], cnt % EPOCH + 1
                cnt += 1

    DUMP = cfg.get("DUMP", False)

    def emit(eng_name, eng):
        for i in P.q[eng_name]:
            if DUMP:
                print(eng_name, i.idx, "waits", [(d.eng, d.idx, str(d.sem), d.val, d.dma) for d in i.waits],
                      "fn" if i.fn else "barrier", "dma" if i.dma else "", "sig" if i.signal else "", str(i.sem), i.val)
            for d in i.waits:
                eng.wait_ge(d.sem, d.val)
            if i.fn is None:
                continue
            ins = i.fn(eng)
            if i.dma:
                ins.then_inc(i.sem, 16)
            elif i.signal:
                ins.then_inc(i.sem, 1)

    with nc.Block() as block:
        @block.sync
        def _(e):
            emit("sp", e)

        @block.scalar
        def _(e):
            emit("act", e)

        @block.vector
        def _(e):
            emit("dve", e)

        @block.gpsimd
        def _(e):
            emit("pool", e)

        @block.tensor
        def _(e):
            emit("pe", e)
    es.close()
    return nc


def host_consts():
    c = np.zeros((128, NK), np.float32)
    j = np.arange(128)[:, None]
    i = np.arange(128)[None, :]
    c[:, K_ID:K_ID + 128] = (i == j)
    c[:, K_TRI:K_TRI + 128] = (j <= i)
    c[:, K_ONE:K_ONE + 128] = 1.0
    c[:, K_MNEG:K_MNEG + 128] = np.where(i >= j, 0.0, -30000.0)
    c[:, K_STR:K_STR + 128] = (i > j)
    for h in range(8):
        c[h, K_IND + h * 128:K_IND + (h + 1) * 128] = 1.0
    return c


def host_cvec(inp, depth):
    cv = np.zeros((depth, 128, NCV), np.float32)
    for l in range(depth):
        def fm(v, n):
            return np.ascontiguousarray(np.asarray(v, np.float32).reshape(n, 128).T)
        cv[l, :, O_MIXG:O_MIXG + 16] = fm(inp["mix_norm_g"][l], 16)
        cv[l, :, O_FFNG:O_FFNG + 16] = fm(inp["ffn_norm_g"][l], 16)
        cv[l, :, O_FING:O_FING + 16] = fm(inp["final_norm_g"], 16)
        cv[l, :, O_DWW:O_DWW + 248] = np.asarray(inp["conv_dw_w"][l], np.float32).reshape(31, 8, 128).transpose(2, 1, 0).reshape(128, 248)
        cv[l, :, O_DWB:O_DWB + 8] = fm(inp["conv_dw_b"][l], 8)
        cv[l, :, O_LNG:O_LNG + 8] = fm(inp["conv_ln_g"][l], 8)
        cv[l, :, O_LNB:O_LNB + 8] = fm(inp["conv_ln_b"][l], 8)
        cv[l, :, O_PWB:O_PWB + 8] = fm(inp["conv_pw_b"][l], 8)
        cv[l, :, O_GCW:O_GCW + 64] = np.asarray(inp["gdn_conv_w"][l], np.float32).reshape(4, 16, 128).transpose(2, 1, 0).reshape(128, 64)
        cv[l, :, O_ALOG:O_ALOG + 8] = np.asarray(inp["gdn_a_log"][l], np.float32)[None, :]
        cv[l, :, O_DTB:O_DTB + 8] = np.asarray(inp["gdn_dt_bias"][l], np.float32)[None, :]
        cv[l, :, O_GNG] = np.asarray(inp["gdn_norm_g"][l], np.float32)
        cv[l, :, O_FCW:O_FCW + 132] = np.asarray(inp["ffn_conv_w"][l], np.float32).reshape(3, NFC, 128).transpose(2, 1, 0).reshape(128, 132)
        cv[l, :, O_FCB:O_FCB + NFC] = fm(inp["ffn_conv_b"][l], NFC)
    return cv


def run(inp, cfg, ncores):
    depth = cfg["DEPTH"]
    nc = build(cfg)
    cst = host_consts()
    cvv = host_cvec(inp, depth)
    f = lambda k: np.ascontiguousarray(np.asarray(inp[k], np.float32)[:depth])
    shared = dict(w_in=f("w_in"), w_pw=f("conv_pw_w"), w_out=f("w_out"), w_up=f("w_up"), w_dn=f("w_down"),
                  cvec=cvv, consts=cst)
    if cfg.get("STOP", 99) <= -2:
        shared = dict(cvec=cvv, consts=cst)
    if cfg.get("WSET", "all") == "in":
        shared = dict(w_in=f("w_in"), cvec=cvv, consts=cst)
    xs = np.asarray(inp["x"], np.float32)
    ns = cfg["NSEQ"]
    in_maps = []
    for c in range(ncores):
        m = dict(shared)
        m["x"] = np.ascontiguousarray(xs[c * ns:(c + 1) * ns])
        in_maps.append(m)
    res = run_bass_kernel_spmd(nc, in_maps, core_ids=list(range(ncores)))
    return np.concatenate([np.asarray(r["out"]) for r in res.results], axis=0)


def kernel(**inputs):
    return run(inputs, CFG, NCORES).astype(np.float32)
```
